# Optimizing a Trainium2 kernel written in Bass

```python
import math
import jax, jax.numpy as jnp
from jax import lax
import numpy as np

D_MODEL = 1024
BATCH = 4
SEQ = 4096
DEPTH = 2
DEC_BATCH = 128
DEC_SEQ = 1
PAST_LEN = 8192
PAGE_SIZE = 128

N_AB_LAYERS = (DEPTH + 1) // 2
N_C_LAYERS = DEPTH // 2
A_HEADS = 8
A_KV_HEADS = 2
A_HEAD_DIM = 64
A_GROUP = A_HEADS // A_KV_HEADS
WINDOW = 128
ATTN_SCALE = A_HEAD_DIM ** -0.5
NUM_BUCKETS = 32
MAX_DISTANCE = 128
A_Q = A_HEADS * A_HEAD_DIM
A_KV = A_KV_HEADS * A_HEAD_DIM
B_GROUPS = 8
B_GROUP_DIM = 64
B_WIDTH = B_GROUPS * B_GROUP_DIM
B_CHUNK = 128
AB_IN = A_Q + 2 * A_KV + 2 * B_WIDTH
AB_MIX = A_Q + B_WIDTH
C_HEADS = 8
C_KEY_DIM = 128
C_VAL_DIM = D_MODEL // C_HEADS
C_F = C_HEADS * C_KEY_DIM
C_V = C_HEADS * C_VAL_DIM
C_IN = 2 * C_F + 2 * C_V
C_CHUNK = 32
D_FF = ((8 * D_MODEL + 767) // 768) * 256
EPS = 1e-6

kernel_name = 'hybrid_swa_gmlp_hgrn2_step'


def _rms(x, g):
    xf = x.astype(jnp.float32)
    y = xf * lax.rsqrt(jnp.mean(xf * xf, axis=-1, keepdims=True) + EPS)
    return (y * g.astype(jnp.float32)).astype(x.dtype)


def _layernorm(x, g, b):
    xf = x.astype(jnp.float32)
    xc = xf - jnp.mean(xf, axis=-1, keepdims=True)
    y = xc * lax.rsqrt(jnp.mean(xc * xc, axis=-1, keepdims=True) + EPS)
    return (y * g.astype(jnp.float32) + b.astype(jnp.float32)).astype(x.dtype)


def _t5_bucket(dist):
    max_exact = NUM_BUCKETS // 2
    d = jnp.maximum(dist, 0)
    dl = jnp.maximum(d, 1).astype(jnp.float32)
    large = max_exact + (jnp.log(dl / max_exact) / math.log(MAX_DISTANCE / max_exact)
                         * (NUM_BUCKETS - max_exact)).astype(jnp.int32)
    large = jnp.minimum(large, NUM_BUCKETS - 1)
    return jnp.where(d < max_exact, d, large)


def _band_bias(dist, ok, rel_bias):
    ok = ok & (dist >= 0) & (dist < WINDOW)
    tab = rel_bias.astype(jnp.float32)[_t5_bucket(dist)]
    tab = jnp.moveaxis(tab, -1, -3)
    b = jnp.where(ok[..., None, :, :], tab, -jnp.inf)
    return b.reshape(b.shape[:-3] + (A_KV_HEADS, A_GROUP) + b.shape[-2:])


def _sink_attend(q, k, v, bias, sink):
    s = jnp.einsum('...qhgd,...khd->...hgqk', q, k, preferred_element_type=jnp.float32) * ATTN_SCALE + bias
    sk = jnp.broadcast_to(sink.astype(jnp.float32).reshape(A_KV_HEADS, A_GROUP, 1, 1), s.shape[:-1] + (1,))
    p = jax.nn.softmax(jnp.concatenate([s, sk], axis=-1), axis=-1)[..., :-1]
    return jnp.einsum('...hgqk,...khd->...qhgd', p.astype(v.dtype), v)


def _ab_project(h, w_in, q_norm, k_norm):
    bn, L = h.shape[:2]
    z = h @ w_in
    q, k, v, zu, zv = jnp.split(z, [A_Q, A_Q + A_KV, A_Q + 2 * A_KV, A_Q + 2 * A_KV + B_WIDTH], axis=-1)
    q = _rms(q.reshape(bn, L, A_HEADS, A_HEAD_DIM), q_norm)
    k = _rms(k.reshape(bn, L, A_KV_HEADS, A_HEAD_DIM), k_norm)
    v = v.reshape(bn, L, A_KV_HEADS, A_HEAD_DIM)
    return q, k, v, zu, zv


def _swa_prompt(q, k, v, rel_bias, sink):
    bn, L = q.shape[:2]
    nb = L // WINDOW
    qb = q.reshape(bn, nb, WINDOW, A_KV_HEADS, A_GROUP, A_HEAD_DIM)

    def band(x):
        xp = jnp.pad(x, ((0, 0), (WINDOW, 0), (0, 0), (0, 0))).reshape(bn, nb + 1, WINDOW, A_KV_HEADS, A_HEAD_DIM)
        return jnp.concatenate([xp[:, :-1], xp[:, 1:]], axis=2)

    qi = jnp.arange(WINDOW)
    kj = jnp.arange(2 * WINDOW)
    dist = qi[:, None] + WINDOW - kj[None, :]
    kpos = jnp.arange(nb)[:, None] * WINDOW - WINDOW + kj[None, :]
    bias = _band_bias(dist, (kpos >= 0)[:, None, :], rel_bias)
    o = _sink_attend(qb, band(k), band(v), bias, sink)
    return o.reshape(bn, L, A_Q)


def _swa_step(q, k_new, v_new, cache_k, cache_v, rel_bias, sink):
    bn, L = q.shape[:2]
    wb = cache_k.shape[1]
    k_all = jnp.concatenate([cache_k.astype(k_new.dtype), k_new], axis=1)
    v_all = jnp.concatenate([cache_v.astype(v_new.dtype), v_new], axis=1)
    qpos = PAST_LEN + jnp.arange(L)
    kpos = PAST_LEN - wb + jnp.arange(wb + L)
    bias = _band_bias(qpos[:, None] - kpos[None, :], (kpos >= 0)[None, :], rel_bias)
    o = _sink_attend(q.reshape(bn, L, A_KV_HEADS, A_GROUP, A_HEAD_DIM), k_all, v_all, bias, sink)
    return o.reshape(bn, L, A_Q), k_all[:, L:], v_all[:, L:]


def _chunk_gmlp(zu, zv, ln_g, ln_b, w_s, b_s):
    bn, L = zu.shape[:2]
    u = jax.nn.gelu(zu, approximate=False)
    v = _layernorm(jax.nn.gelu(zv, approximate=False), ln_g, ln_b)
    nc = -(-L // B_CHUNK)
    pad = nc * B_CHUNK - L
    vp = jnp.pad(v, ((0, 0), (0, pad), (0, 0))).reshape(bn, nc, B_CHUNK, B_GROUPS, B_GROUP_DIM)
    w = jnp.where(jnp.tril(jnp.ones((B_CHUNK, B_CHUNK), bool)), w_s, 0.0).astype(v.dtype)
    s = jnp.einsum('gts,bcsgd->bctgd', w, vp) + b_s.T.astype(v.dtype)[:, :, None]
    s = s.reshape(bn, nc * B_CHUNK, B_WIDTH)[:, :L]
    open_start = ((L - 1) // B_CHUNK) * B_CHUNK
    return u * s, v[:, open_start:]


def _hgrn2_scan(q, fl, i, lb, s0):
    bn, L = q.shape[:2]
    f = lb + (1.0 - lb) * jax.nn.sigmoid(fl.astype(jnp.float32))
    g = jnp.log(f)
    kk = 1.0 - f
    c = min(C_CHUNK, L)
    nc = -(-L // c)
    pad = nc * c - L

    def chunks(x):
        x = jnp.pad(x.astype(jnp.float32), ((0, 0), (0, pad), (0, 0), (0, 0)))
        return x.reshape(bn, nc, c, x.shape[2], x.shape[3]).transpose(1, 0, 3, 2, 4)

    mask = jnp.tril(jnp.ones((c, c), bool))[:, :, None]

    def step(S, xs):
        qc, kc, vc, gc = xs
        b = jnp.cumsum(gc, axis=2)
        o = jnp.einsum('bhtd,bhde->bhte', qc * jnp.exp(b), S)
        decay = jnp.exp(jnp.where(mask, b[:, :, :, None, :] - b[:, :, None, :, :], -jnp.inf))
        att = jnp.einsum('bhtd,bhsd,bhtsd->bhts', qc, kc, decay)
        o = o + jnp.einsum('bhts,bhse->bhte', att, vc)
        b_last = b[:, :, -1:, :]
        S = jnp.exp(b_last[:, :, 0, :])[..., None] * S + jnp.einsum('bhsd,bhse->bhde', kc * jnp.exp(b_last - b), vc)
        return S, o

    S, o = lax.scan(step, s0.astype(jnp.float32), (chunks(q), chunks(kk), chunks(i), chunks(g)))
    o = o.transpose(1, 0, 3, 2, 4).reshape(bn, nc * c, q.shape[2], i.shape[3])[:, :L]
    return o, S


def _hgrn_mixer(h, w_in, lb_l, out_norm, w_out, s0):
    bn, L = h.shape[:2]
    z = h @ w_in
    q, fl, i, gt = jnp.split(z, [C_F, 2 * C_F, 2 * C_F + C_V], axis=-1)
    o, S = _hgrn2_scan(q.reshape(bn, L, C_HEADS, C_KEY_DIM), fl.reshape(bn, L, C_HEADS, C_KEY_DIM),
                       i.reshape(bn, L, C_HEADS, C_VAL_DIM), lb_l.reshape(C_HEADS, C_KEY_DIM), s0)
    o = _rms(o.astype(h.dtype), out_norm) * jax.nn.sigmoid(gt).reshape(bn, L, C_HEADS, C_VAL_DIM)
    return o.reshape(bn, L, C_V) @ w_out, S


def _ffn(x, g, w_gate, w_up, w_down):
    h = _rms(x, g)
    return x + (jax.nn.silu(h @ w_gate) * (h @ w_up)) @ w_down


def setup_inputs(seed: int = 0) -> dict:
    key = jax.random.key(seed)
    ks = jax.random.split(key, 24)

    def nrm(k, shape, scale):
        return scale * jax.random.normal(k, shape, jnp.float32)

    wbuf = min(WINDOW, PAST_LEN)
    return {
        'x_prompt': nrm(ks[0], (BATCH, SEQ, D_MODEL), 1.0),
        'x_sample': nrm(ks[1], (DEC_BATCH, DEC_SEQ, D_MODEL), 1.0),
        'cache_k': nrm(ks[2], (N_AB_LAYERS, DEC_BATCH, wbuf, A_KV_HEADS, A_HEAD_DIM), 1.0),
        'cache_v': nrm(ks[3], (N_AB_LAYERS, DEC_BATCH, wbuf, A_KV_HEADS, A_HEAD_DIM), 1.0),
        'state_hgrn': nrm(ks[4], (N_C_LAYERS, DEC_BATCH, C_HEADS, C_KEY_DIM, C_VAL_DIM), 0.5),
        'norm_mix': 1.0 + nrm(ks[5], (DEPTH, D_MODEL), 0.05),
        'norm_ffn': 1.0 + nrm(ks[6], (DEPTH, D_MODEL), 0.05),
        'w_in_ab': nrm(ks[7], (N_AB_LAYERS, D_MODEL, AB_IN), D_MODEL ** -0.5),
        'w_out_ab': nrm(ks[8], (N_AB_LAYERS, AB_MIX, D_MODEL), AB_MIX ** -0.5),
        'q_norm': 1.0 + nrm(ks[9], (N_AB_LAYERS, A_HEAD_DIM), 0.05),
        'k_norm': 1.0 + nrm(ks[10], (N_AB_LAYERS, A_HEAD_DIM), 0.05),
        'attn_sink': nrm(ks[11], (N_AB_LAYERS, A_HEADS), 0.5),
        'rel_bias': nrm(ks[12], (NUM_BUCKETS, A_HEADS), 0.5),
        'gmlp_ln_g': 1.0 + nrm(ks[13], (N_AB_LAYERS, B_WIDTH), 0.05),
        'gmlp_ln_b': nrm(ks[14], (N_AB_LAYERS, B_WIDTH), 0.02),
        'gmlp_w_s': nrm(ks[15], (N_AB_LAYERS, B_GROUPS, B_CHUNK, B_CHUNK), B_CHUNK ** -0.5),
        'gmlp_b_s': 1.0 + nrm(ks[16], (N_AB_LAYERS, B_GROUPS, B_CHUNK), 0.1),
        'w_in_c': nrm(ks[17], (N_C_LAYERS, D_MODEL, C_IN), D_MODEL ** -0.5),
        'c_lower_bounds': nrm(ks[18], (DEPTH, C_F), 0.1),
        'c_out_norm': 1.0 + nrm(ks[19], (N_C_LAYERS, C_VAL_DIM), 0.05),
        'w_out_c': nrm(ks[20], (N_C_LAYERS, C_V, D_MODEL), C_V ** -0.5),
        'w_gate': nrm(ks[21], (DEPTH, D_MODEL, D_FF), D_MODEL ** -0.5),
        'w_up': nrm(ks[22], (DEPTH, D_MODEL, D_FF), D_MODEL ** -0.5),
        'w_down': nrm(ks[23], (DEPTH, D_FF, D_MODEL), D_FF ** -0.5),
    }


def reference(x_prompt, x_sample, cache_k, cache_v, state_hgrn,
              norm_mix, norm_ffn, w_in_ab, w_out_ab, q_norm, k_norm, attn_sink, rel_bias,
              gmlp_ln_g, gmlp_ln_b, gmlp_w_s, gmlp_b_s,
              w_in_c, c_lower_bounds, c_out_norm, w_out_c,
              w_gate, w_up, w_down):
    lb = jax.nn.softmax(c_lower_bounds.astype(jnp.float32), axis=0)
    lb = jnp.cumsum(lb, axis=0) - lb[0:1]

    xp, xs = x_prompt, x_sample
    kp_l, vp_l, ks_l, vs_l = [], [], [], []
    gvp_l, gvs_l, sp_l, ss_l = [], [], [], []
    for l in range(DEPTH):
        j = l // 2
        hp = _rms(xp, norm_mix[l])
        hs = _rms(xs, norm_mix[l])
        if l % 2 == 0:
            q, k, v, zu, zv = _ab_project(hp, w_in_ab[j], q_norm[j], k_norm[j])
            a = _swa_prompt(q, k, v, rel_bias, attn_sink[j])
            b, gv = _chunk_gmlp(zu, zv, gmlp_ln_g[j], gmlp_ln_b[j], gmlp_w_s[j], gmlp_b_s[j])
            xp = xp + jnp.concatenate([a, b], axis=-1) @ w_out_ab[j]
            nwin = min(WINDOW, k.shape[1])
            kp_l.append(k[:, -nwin:])
            vp_l.append(v[:, -nwin:])
            gvp_l.append(gv)
            q, k, v, zu, zv = _ab_project(hs, w_in_ab[j], q_norm[j], k_norm[j])
            a, kw, vw = _swa_step(q, k, v, cache_k[j], cache_v[j], rel_bias, attn_sink[j])
            b, gv = _chunk_gmlp(zu, zv, gmlp_ln_g[j], gmlp_ln_b[j], gmlp_w_s[j], gmlp_b_s[j])
            xs = xs + jnp.concatenate([a, b], axis=-1) @ w_out_ab[j]
            ks_l.append(kw)
            vs_l.append(vw)
            gvs_l.append(gv)
        else:
            s0 = jnp.zeros((xp.shape[0], C_HEADS, C_KEY_DIM, C_VAL_DIM), jnp.float32)
            o, s = _hgrn_mixer(hp, w_in_c[j], lb[l], c_out_norm[j], w_out_c[j], s0)
            xp = xp + o
            sp_l.append(s.astype(xp.dtype))
            o, s = _hgrn_mixer(hs, w_in_c[j], lb[l], c_out_norm[j], w_out_c[j], state_hgrn[j])
            xs = xs + o
            ss_l.append(s.astype(state_hgrn.dtype))
        xp = _ffn(xp, norm_ffn[l], w_gate[l], w_up[l], w_down[l])
        xs = _ffn(xs, norm_ffn[l], w_gate[l], w_up[l], w_down[l])

    new_k_prompt = jnp.stack(kp_l)
    new_v_prompt = jnp.stack(vp_l)
    new_k_sample = jnp.stack(ks_l)
    new_v_sample = jnp.stack(vs_l)
    gmlp_v_prompt = jnp.stack(gvp_l)
    gmlp_v_sample = jnp.stack(gvs_l)
    state_hgrn_prompt = jnp.stack(sp_l)
    state_hgrn_sample = jnp.stack(ss_l)
    return (xp, xs, new_k_prompt, new_v_prompt, new_k_sample, new_v_sample,
            gmlp_v_prompt, gmlp_v_sample, state_hgrn_prompt, state_hgrn_sample)
```

```python
from contextlib import ExitStack
import math
import numpy as np
import concourse.bass as bass
import concourse.mybir as mybir
from concourse.bass_utils import run_bass_kernel_spmd

F32 = mybir.dt.float32
BF16 = mybir.dt.bfloat16
ALU = mybir.AluOpType
AF = mybir.ActivationFunctionType

NCORES = 8
D = 1024
NBLK = 9
NTP = NBLK * 128
NS = 16
NTA = NTP + NS
RING = 3
NEG = -30000.0
ARF = 12960
ARB = 12600
EPS = 1e-6
DFF = 2816
KG = (8, 8, 6)


class Res:
    __slots__ = ("name", "writer", "readers", "sem", "semcnt")

    def __init__(self, name):
        self.name = name
        self.writer = None
        self.readers = []
        self.sem = None
        self.semcnt = 0


class Op:
    __slots__ = ("eng", "fn", "idx", "deps", "is_dma", "sem", "val", "need_inc")


class Sched:
    ENGS = ("pe", "act", "dve", "pool", "sp")

    def __init__(self):
        self.ops = {e: [] for e in self.ENGS}

    def seq(self, eng, fns, reads=(), writes=()):
        op = None
        for f in fns:
            op = self.add(eng, f, reads=reads, writes=writes)
        return op

    def add(self, eng, fn, reads=(), writes=(), dma_owner=None, extra_deps=()):
        op = Op()
        op.eng = eng
        op.fn = fn
        op.idx = len(self.ops[eng])
        op.is_dma = dma_owner is not None
        op.need_inc = op.is_dma
        op.sem = None
        op.val = None
        deps = list(extra_deps)
        for r in reads:
            if r.writer is not None:
                deps.append(r.writer)
        for w in writes:
            if w.writer is not None:
                deps.append(w.writer)
            deps.extend(w.readers)
        for r in reads:
            r.readers.append(op)
        for w in writes:
            w.writer = op
            w.readers = []
        op.deps = [d for d in set(deps) if d is not op]
        if op.is_dma:
            dma_owner.semcnt += 16
            op.sem = dma_owner
            op.val = dma_owner.semcnt
        self.ops[eng].append(op)
        return op

    @staticmethod
    def _needs_sync(op, d):
        if d.is_dma or op.is_dma:
            return True
        if d.eng != op.eng:
            return True
        if op.eng == "pe":
            return False
        return (op.idx - d.idx) <= 2

    def finalize(self, sems):
        for e in self.ENGS:
            for op in self.ops[e]:
                for d in op.deps:
                    if self._needs_sync(op, d) and not d.is_dma:
                        d.need_inc = True
        for e in self.ENGS:
            cnt = 0
            for op in self.ops[e]:
                if op.is_dma:
                    continue
                if op.need_inc:
                    cnt += 1
                    op.val = cnt
        self._sems = sems

    def run_engine(self, e, eng):
        sems = self._sems
        known = {}
        for op in self.ops[e]:
            waits = {}
            for d in op.deps:
                if not self._needs_sync(op, d):
                    continue
                s = d.sem.sem if d.is_dma else sems[d.eng]
                v = d.val
                key = id(s)
                if known.get(key, 0) >= v:
                    continue
                if key not in waits or waits[key][1] < v:
                    waits[key] = (s, v)
            for key, (s, v) in waits.items():
                eng.wait_ge(s, v)
                known[key] = v
            ins = op.fn(eng)
            if op.is_dma:
                ins.then_inc(op.sem.sem, 16)
            elif op.need_inc:
                ins.then_inc(sems[e], 1)


def _t5_bucket_np(dist):
    d = np.maximum(dist, 0)
    dl = np.maximum(d, 1).astype(np.float32)
    val = (np.log(dl / np.float32(16)) / np.float32(math.log(128 / 16)) * np.float32(16)).astype(np.float32)
    large = 16 + val.astype(np.int32)
    large = np.minimum(large, 31)
    return np.where(d < 16, d, large)


def _wtile(W, rows, cols):
    t = np.zeros((1024, 128), np.float32)
    sub = W[np.asarray(rows)][:, np.asarray(cols)]
    t[: sub.shape[0], : sub.shape[1]] = sub
    return t.reshape(8, 128, 128).transpose(1, 0, 2)


def build_weight_stream(w_in_ab, w_out_ab, w_in_c, w_out_c, w_gate, w_up, w_down):
    tiles = []
    r1024 = np.arange(1024)
    zero = np.zeros((128, 8, 128), np.float32)

    def ffn(l):
        for kg, nch in enumerate(KG):
            base = sum(KG[:kg])
            for j in range(nch):
                c = np.arange((base + j) * 128, (base + j + 1) * 128)
                tiles.append(_wtile(w_gate[l], r1024, c))
                tiles.append(_wtile(w_up[l], r1024, c))
            for oc in range(8):
                rows = np.arange(base * 128, (base + nch) * 128)
                tiles.append(_wtile(w_down[l], rows, np.arange(oc * 128, (oc + 1) * 128)))

    Wi = w_in_ab[0]
    for c in range(4):
        cols = np.concatenate([np.arange(c * 64, c * 64 + 64), np.arange((4 + c) * 64, (4 + c) * 64 + 64)])
        tiles.append(_wtile(Wi, r1024, cols))
    tiles.append(_wtile(Wi, r1024, np.arange(512, 640)))
    tiles.append(_wtile(Wi, r1024, np.arange(640, 768)))
    tiles.append(zero)
    tiles.append(zero)
    for c in range(4):
        tiles.append(_wtile(Wi, r1024, np.arange(768 + c * 128, 768 + (c + 1) * 128)))
    for c in range(4):
        tiles.append(_wtile(Wi, r1024, np.arange(1280 + c * 128, 1280 + (c + 1) * 128)))
    Wo = w_out_ab[0]
    rows = []
    for c in range(4):
        rows.append(np.arange(c * 64, c * 64 + 64))
        rows.append(np.arange((4 + c) * 64, (4 + c) * 64 + 64))
    rows.append(np.arange(512, 1024))
    rows = np.concatenate(rows)
    for oc in range(8):
        tiles.append(_wtile(Wo, rows, np.arange(oc * 128, (oc + 1) * 128)))
    ffn(0)
    Wc = w_in_c[0]
    for h in range(8):
        for part in range(4):
            tiles.append(_wtile(Wc, r1024, np.arange(part * 1024 + h * 128, part * 1024 + (h + 1) * 128)))
    for oc in range(8):
        tiles.append(_wtile(w_out_c[0], r1024, np.arange(oc * 128, (oc + 1) * 128)))
    ffn(1)
    assert len(tiles) % 4 == 0
    arr = np.stack(tiles).reshape(len(tiles) // 4, 4, 128, 8, 128).transpose(0, 2, 1, 3, 4)
    return np.ascontiguousarray(arr)


NPAR = 80
PC = dict(nm0=0, nf0=8, nm1=16, nf1=24, qn=32, kn=33, c0=34, c1=42, on=50, w00=51, bs0=55)


def build_params(norm_mix, norm_ffn, q_norm, k_norm, c_lower_bounds, c_out_norm, gmlp_w_s, gmlp_b_s):
    par = np.zeros((128, NPAR), np.float32)
    fm = lambda v: v.reshape(8, 128).T
    par[:, 0:8] = fm(norm_mix[0])
    par[:, 8:16] = fm(norm_ffn[0])
    par[:, 16:24] = fm(norm_mix[1])
    par[:, 24:32] = fm(norm_ffn[1])
    par[:, 32] = np.tile(q_norm[0], 2)
    par[:, 33] = np.tile(k_norm[0], 2)
    par[:, 34:42] = fm(c_lower_bounds[0])
    par[:, 42:50] = fm(c_lower_bounds[1])
    par[:, 50] = c_out_norm[0]
    for c in range(4):
        for j in range(2):
            par[j * 64:(j + 1) * 64, 51 + c] = gmlp_w_s[0, 2 * c + j, 0, 0]
            par[j * 64:(j + 1) * 64, 55 + c] = gmlp_b_s[0, 2 * c + j, 0]
    return par


def build_consts():
    ohA = np.zeros((33, 384), np.float32)
    for j in range(384):
        dist = j - 128
        if 0 <= dist < 128:
            ohA[int(_t5_bucket_np(np.array(dist))), j] = 1.0
        else:
            ohA[32, j] = 1.0
    ohS = np.zeros((33, 128), np.float32)
    for j in range(128):
        ohS[int(_t5_bucket_np(np.array(127 - j))), j] = 1.0
    return ohA, ohS


def build_program(NG):
    nc = bass.Bass("TRN2", target_bir_lowering=False)
    din = lambda name, shape: nc.dram_tensor(name, list(shape), F32, kind="ExternalInput").ap()
    dout = lambda name, shape: nc.dram_tensor(name, list(shape), F32, kind="ExternalOutput").ap()
    xin = din("xin", [2 * NTP + NS, D])
    hm_d = din("hm", [128, 1])
    ck_d = din("ck", [NS, 127, 128])
    cv_d = din("cv", [NS, 127, 128])
    sin_d = din("sin", [NS, 8, 128, 128])
    wall = din("wall", [NG, 128, 4, 8, 128])
    par_d = din("par", [128, NPAR])
    bc_d = din("bcast", [128, 1024])
    relb_d = din("relb", [32, 8])
    sink_d = din("sink", [1, 8])
    ohA_d = din("ohA", [33, 384])
    ohS_d = din("ohS", [33, 128])
    wsT_d = din("wsT", [128, 8, 128])
    bsr_d = din("bsr", [1, 8, 128])
    y_d = dout("y", [2048, D])
    ys_d = dout("ys", [NS, D])
    nk_d = dout("nk", [128, 128])
    nv_d = dout("nv", [128, 128])
    nks_d = dout("nks", [NS, 128, 128])
    nvs_d = dout("nvs", [NS, 128, 128])
    gv_d = dout("gv", [128, 512])
    gvs_d = dout("gvs", [NS, 512])
    sp_d = dout("stp", [8, 128, 128])
    ss_d = dout("sts", [NS, 8, 128, 128])
    abuf = nc.dram_tensor("abuf", [8, 384], F32).ap()

    es = ExitStack()
    with es:
        sb = lambda name, shape, dt: es.enter_context(nc.sbuf_tensor(name, list(shape), dt))
        S = Sched()
        allres = []
        semctr = [0]

        def R(name, dma=False):
            r = Res(name)
            if dma:
                semctr[0] += 1
                r.sem = es.enter_context(nc.semaphore("d%d_%s" % (semctr[0], name)))
            return r

        XT = sb("XT", [128, 8, NTA], F32)
        hT = sb("hT", [128, 8, NTA], BF16)
        aT = sb("aT", [128, 8, NTA], BF16)
        ring = sb("ring", [128, RING, 4, 8, 128], BF16)
        rp = sb("rp", [128, 2, 512], F32)
        sgt = sb("sgt", [128, 2, 512], BF16)
        knT = sb("knT", [128, 10, 128], BF16)
        Vb = sb("Vb", [128, 10, 128], BF16)
        Sst = sb("Sst", [128, 8, 128], F32)
        Tb = [sb("Tb%d" % i, [128, 8, 128], BF16) for i in range(2)]
        identB = sb("identB", [128, 128], BF16)
        WsT = sb("WsT", [128, 8, 128], BF16)
        identF = sb("identF", [128, 128], F32)
        onesB = sb("onesB", [128, 128], BF16)
        onesF = sb("onesF", [128, 128], F32)
        blkB = sb("blkB", [128, 128], BF16)
        maskT = sb("maskT", [128, 128], F32)
        Jm = sb("Jm", [128, 128], F32)
        par = sb("par_sb", [128, NPAR], F32)
        pp = sb("pp", [128, 64], F32)
        bcast = sb("bcast_sb", [128, 1024], F32)
        hmc = sb("hmc_sb", [128, 1], F32)
        epsc = sb("epsc", [128, 4], F32)
        relx = sb("relx", [33, 8], F32)
        ohS = sb("ohS_sb", [33, 128], F32)
        biasS = sb("biasS", [128, 8], F32)
        sinkr = sb("sinkr", [1, 8], F32)
        esrow = sb("esrow", [1, 8, 128], BF16)
        es8 = sb("es8", [1, 8], BF16)
        bsrow = sb("bsrow", [1, 8, 128], BF16)
        bsG = sb("bsG", [128, 512], F32)
        arF = sb("arF", [128, ARF], F32)
        xtok = arF[:, 5120:7168].rearrange("p (i f) -> p i f", i=2)
        trev = arF[:, 0:2048].rearrange("p (i h q) -> p i h q", i=2, h=8)
        wsf = arF[:, 2048:3072].rearrange("p (g t) -> p g t", g=8)
        bsrf = arF[0:1, 3072:4096].rearrange("p (g t) -> p g t", g=8)
        ohA = arF[0:33, 4096:4480]
        A8 = arF[0:8, 4480:4864]
        arB = sb("arB", [128, ARB], BF16)
        banks = [es.enter_context(nc.psum_tensor("bank%d" % i, [128, 512], F32)) for i in range(8)]
        r_bank = [R("bank%d" % i) for i in range(8)]
        bank_i = [0]

        def nb():
            i = bank_i[0] % 8
            bank_i[0] += 1
            return banks[i], r_bank[i]

        sems = {e: es.enter_context(nc.semaphore("sem_" + e)) for e in Sched.ENGS}

        class Carver:
            def __init__(self, buf, cap):
                self.buf, self.cap, self.off = buf, cap, 0

            def get(self, n):
                a = self.buf[:, self.off:self.off + n]
                self.off += n
                assert self.off <= self.cap, (self.off, self.cap)
                return a

        def arena_res(name):
            r = R(name)
            allres.append(r)
            return r

        def barrier():
            deps = []
            for r in allres:
                if r.writer is not None:
                    deps.append(r.writer)
                deps.extend(r.readers)
            for eng in ("pe", "act", "pool", "sp"):
                S.add(eng, lambda e: e.nop(), extra_deps=deps)
            S.add("dve", lambda e: e.nop(), writes=list(allres))

        TANH_SETS = ("exp0", "sig", "gelu", "silu")
        cur_set = ["?"]

        def act6():
            return

        def act_uses(name):
            if name == "tanh":
                if cur_set[0] not in TANH_SETS:
                    cur_set[0] = "exp0"
            else:
                cur_set[0] = name

        def rsqrt_act(dst, src_ps, eps_col, reads, r_dst):
            S.add("act", lambda e: e.activation(out=dst, in_=src_ps, func=AF.Sqrt, bias=eps_col), reads=reads + [r_cc], writes=[r_dst])
            S.add("dve", lambda e: e.reciprocal(out=dst, in_=dst), writes=[r_dst])

        r_c = R("consts", dma=True)
        r_cc = R("constsc")
        cl = [(par, par_d), (bcast, bc_d), (hmc, hm_d), (relx[0:32, :], relb_d), (ohA, ohA_d), (ohS, ohS_d),
              (wsf, wsT_d), (sinkr, sink_d), (bsrf, bsr_d)]
        r_setup = arena_res("setup_alias")
        for dst, src in cl:
            S.add("sp", lambda e, dst=dst, src=src: e.dma_start(out=dst[:] if not isinstance(dst, bass.AP) else dst, in_=src),
                  writes=[r_c, r_setup], dma_owner=r_c)

        for g in range(8):
            src = bass.AP(bsr_d.tensor, g * 128, [[0, 64], [1, 128]])
            S.add("sp", lambda e, g=g, src=src: e.dma_start(out=bsG[(g % 2) * 64:(g % 2) * 64 + 64, (g // 2) * 128:(g // 2 + 1) * 128], in_=src),
                  writes=[r_c], dma_owner=r_c)
        S.seq("pool", [
            lambda e: e.memset(identF[:], 1.0),
            lambda e: e.affine_select(out=identF[:], in_=identF[:], compare_op=ALU.is_ge, fill=0.0, base=0, pattern=[[-1, 128]], channel_multiplier=1),
            lambda e: e.affine_select(out=identF[:], in_=identF[:], compare_op=ALU.is_ge, fill=0.0, base=0, pattern=[[1, 128]], channel_multiplier=-1),
            lambda e: e.tensor_copy(out=identB[:], in_=identF[:]),
            lambda e: e.memset(Jm[:], 1.0),
            lambda e: e.affine_select(out=Jm[:], in_=Jm[:], compare_op=ALU.is_ge, fill=0.0, base=-127, pattern=[[1, 128]], channel_multiplier=1),
            lambda e: e.affine_select(out=Jm[:], in_=Jm[:], compare_op=ALU.is_ge, fill=0.0, base=127, pattern=[[-1, 128]], channel_multiplier=-1),
            lambda e: e.memset(onesB[:], 1.0),
            lambda e: e.memset(onesF[:], 1.0),
            lambda e: e.memset(blkB[:], 0.0),
            lambda e: e.memset(blkB[0:64, 0:64], 1.0),
            lambda e: e.memset(blkB[64:128, 64:128], 1.0),
            lambda e: e.memset(maskT[:], 1.0),
            lambda e: e.affine_select(out=maskT[:], in_=maskT[:], compare_op=ALU.is_ge, fill=0.0, base=0, pattern=[[1, 128]], channel_multiplier=-1),
            lambda e: e.memset(Sst[:], 0.0),
            lambda e: e.memset(knT[:], 0.0),
            lambda e: e.memset(Vb[:], 0.0),
            lambda e: e.memset(epsc[:, 0:1], 1024.0 * EPS),
            lambda e: e.memset(epsc[:, 1:2], 64.0 * EPS),
            lambda e: e.memset(epsc[:, 2:3], EPS),
            lambda e: e.memset(epsc[:, 3:4], 128.0 * EPS),
        ], writes=[r_cc])
        S.add("pool", lambda e: e.memset(relx[32:33, :], NEG), writes=[r_c])

        r_c2 = R("c2")
        S.seq("pool", [
            lambda e: e.affine_select(out=wsf, in_=wsf, compare_op=ALU.is_ge, fill=0.0, base=0, pattern=[[0, 8], [1, 128]], channel_multiplier=-1),
            lambda e: e.tensor_copy(out=WsT[:], in_=wsf),
            lambda e: e.tensor_copy(out=bsrow[:], in_=bsrf),
        ], reads=[r_c, r_cc], writes=[r_c2, r_setup])

        PP = dict(g32_0=0, g32f_0=8, g32_1=16, g32f_1=24, lb=32, oml=40, lbm1=48, on=56)
        r_pp = R("pp")

        S.seq("dve", [
            lambda e: e.tensor_scalar(out=pp[:, 0:32], in0=par[:, 0:32], scalar1=32.0, scalar2=None, op0=ALU.mult),
            lambda e: e.tensor_tensor(out=pp[:, 32:40], in0=par[:, PC["c1"]:PC["c1"] + 8], in1=par[:, PC["c0"]:PC["c0"] + 8], op=ALU.subtract),
            lambda e: e.tensor_scalar(out=pp[:, 56:57], in0=par[:, PC["on"]:PC["on"] + 1], scalar1=math.sqrt(128.0), scalar2=None, op0=ALU.mult),
            lambda e: e.tensor_scalar(out=pp[:, 57:58], in0=par[:, PC["kn"]:PC["kn"] + 1], scalar1=8.0, scalar2=None, op0=ALU.mult),
        ], reads=[r_c], writes=[r_pp])
        S.add("act", lambda e: e.activation(out=pp[:, 32:40], in_=pp[:, 32:40], func=AF.Sigmoid), reads=[r_pp], writes=[r_pp])
        act_uses("sig")
        S.seq("dve", [
            lambda e: e.tensor_scalar(out=pp[:, 40:48], in0=pp[:, 32:40], scalar1=-0.5, scalar2=0.5, op0=ALU.mult, op1=ALU.add),
            lambda e: e.tensor_scalar(out=pp[:, 48:56], in0=pp[:, 32:40], scalar1=0.5, scalar2=-0.5, op0=ALU.mult, op1=ALU.add),
            lambda e: e.tensor_scalar(out=pp[:, 32:40], in0=pp[:, 32:40], scalar1=0.5, scalar2=0.5, op0=ALU.mult, op1=ALU.add),
            lambda e: e.tensor_scalar(out=pp[:, 58:59], in0=pp[:, 56:57], scalar1=0.5, scalar2=None, op0=ALU.mult),
        ], reads=[r_pp], writes=[r_pp])

        r_sk = R("sink")
        act6()
        S.add("act", lambda e: e.activation(out=es8[:], in_=sinkr[:], func=AF.Exp), reads=[r_c], writes=[r_sk])
        S.add("act", lambda e: e.activation(out=esrow[:], in_=sinkr[:].unsqueeze(2).broadcast_to([1, 8, 128]), func=AF.Exp),
              reads=[r_c], writes=[r_sk])

        r_tab = R("tab", dma=True)
        r_tab2 = R("tab2")
        bk, rbk = nb()
        S.add("pe", lambda e: e.matmul(bk[0:8, 0:384], lhsT=relx[:, :], rhs=ohA, start=True, stop=True),
              reads=[r_c, r_cc, r_setup], writes=[rbk])
        S.add("dve", lambda e: e.tensor_copy(out=A8, in_=bk[0:8, 0:384]), reads=[rbk], writes=[r_tab2, r_setup])
        S.add("sp", lambda e: e.dma_start(out=abuf, in_=A8), reads=[r_tab2], writes=[r_tab], dma_owner=r_tab)
        for i, off in enumerate((1, 129)):
            src = bass.AP(abuf.tensor, off, [[1, 128], [384, 8], [1, 128]])
            S.add("sp", lambda e, i=i, src=src: e.dma_start(out=trev[:, i], in_=src), reads=[r_tab], writes=[r_tab, r_setup], dma_owner=r_tab)
        for i, T in enumerate((Tb[1], Tb[0])):
            for hh in range(2):
                bk, rbk = nb()
                S.add("pe", lambda e, bk=bk, i=i, hh=hh: e.matmul(bk[:].rearrange("p (c n) -> p c n", c=4), lhsT=Jm[:], rhs=trev[:, i, 4 * hh:4 * hh + 4, :], start=True, stop=True),
                      reads=[r_tab, r_cc, r_setup], writes=[rbk])
                S.add("dve", lambda e, bk=bk, T=T, hh=hh: e.tensor_copy(out=T[:, 4 * hh:4 * hh + 4, :], in_=bk[:].rearrange("p (c n) -> p c n", c=4)),
                      reads=[rbk], writes=[r_tab2])
        bk, rbk = nb()
        S.add("pe", lambda e, bk=bk: e.matmul(bk[:, 0:8], lhsT=ohS[:, :], rhs=relx[:, :], start=True, stop=True), reads=[r_c, r_cc], writes=[rbk])
        S.add("dve", lambda e, bk=bk: e.tensor_copy(out=biasS[:], in_=bk[:, 0:8]), reads=[rbk], writes=[r_tab2])

        r_slot = [R("slot%d" % i, dma=True) for i in range(RING)]
        wstate = dict(issued=0, taken=0)

        def w_issue():
            g = wstate["issued"]
            s = g % RING
            S.add("pool", lambda e, g=g, s=s: e.dma_start(out=ring[:, s], in_=wall[g % NG]), writes=[r_slot[s]], dma_owner=r_slot[s])
            wstate["issued"] += 1

        def w_next():
            g = wstate["taken"]
            while wstate["issued"] < min(g + RING, 2 * NG):
                w_issue()
            wstate["taken"] += 1
            s = g % RING
            return ring[:, s], r_slot[s]

        r_X = [R("X%d" % i) for i in range(10)]
        r_h = [R("h%d" % i) for i in range(10)]
        r_h2 = [R("hb%d" % i) for i in range(10)]

        def rh(bl):
            return [r_h[b] for b in bl] + [r_h2[b] for b in bl]
        r_a = [R("a%d" % i) for i in range(10)]
        r_u = [R("u%d" % i) for i in range(10)]
        r_kn = [R("kn%d" % i) for i in range(10)]
        r_vb = [R("vb%d" % i) for i in range(10)]
        r_xtok = [R("xtok%d" % i, dma=True) for i in range(2)]
        allres.extend(r_xtok)
        r_rp = [R("rp%d" % i) for i in range(2)]
        r_sgt = [R("sgt%d" % i) for i in range(2)]
        r_S = R("Sst", dma=True)
        out_ops = []
        r_onk, r_onv, r_ogv, r_ogvs = [R(n, dma=True) for n in ("onk", "onv", "ogv", "ogvs")]
        cnt = dict(xtok=0, rp=0, sgt=0)

        def blocks_of(c0, n):
            return list(range(c0 // 128, (c0 + n + 127) // 128))

        def rot(name, n=2):
            i = cnt[name] % n
            cnt[name] += 1
            return i

        skipB0 = [False]

        def tiles_of(NT):
            if skipB0[0]:
                s0 = int(skipB0[0])
                return [(s0, 512 - s0), (512, 512), (1024, NT - 1024)]
            return [(0, 512), (512, 512), (1024, NT - 1024)]

        def rmsnorm(NT, gcol):
            tl = tiles_of(NT)

            def sq_stage(c0, n):
                npool = 128 if (n >= 256 and n % 128 == 0) else 0
                na = n - npool
                bla = blocks_of(c0, na)
                S.add("act", lambda e: e.activation(out=aT[:, :, c0:c0 + na], in_=XT[:, :, c0:c0 + na], func=AF.Square),
                      reads=[r_X[b] for b in bla], writes=[r_a[b] for b in bla] + [r_u[b] for b in bla])
                if npool:
                    blp = blocks_of(c0 + na, npool)
                    c1 = c0 + na
                    S.add("pool", lambda e: e.tensor_tensor(out=aT[:, :, c1:c1 + 128], in0=XT[:, :, c1:c1 + 128], in1=XT[:, :, c1:c1 + 128], op=ALU.mult),
                          reads=[r_X[b] for b in blp], writes=[r_a[b] for b in blp] + [r_u[b] for b in blp])

            def rest_stage(c0, n):
                bl = blocks_of(c0, n)
                bk, rbk = nb()

                def mm(e):
                    for kc in range(8):
                        ins = e.matmul(bk[:, 0:n], lhsT=onesB[:], rhs=aT[:, kc, c0:c0 + n], start=(kc == 0), stop=(kc == 7))
                    return ins
                S.add("pe", mm, reads=[r_a[b] for b in bl] + [r_u[b] for b in bl], writes=[rbk])
                i = rot("rp")
                S.add("act", lambda e: e.activation(out=rp[:, i, 0:n], in_=bk[:, 0:n], func=AF.Ln, bias=epsc[:, 0:1]), reads=[rbk, r_cc], writes=[r_rp[i]])
                S.add("act", lambda e: e.activation(out=rp[:, i, 0:n], in_=rp[:, i, 0:n], func=AF.Exp, scale=-0.5), writes=[r_rp[i]])

                def hh(e):
                    for kc in range(8):
                        ins = e.scalar_tensor_tensor(out=hT[:, kc, c0:c0 + n], in0=XT[:, kc, c0:c0 + n], scalar=pp[:, gcol + kc:gcol + kc + 1],
                                                     in1=rp[:, i, 0:n], op0=ALU.mult, op1=ALU.mult)
                    return ins
                S.add("dve", hh, reads=[r_X[b] for b in bl] + [r_rp[i], r_pp], writes=[r_h[b] for b in bl])

            sq_stage(*tl[0])
            for k_ in range(len(tl)):
                if k_ + 1 < len(tl):
                    sq_stage(*tl[k_ + 1])
                rest_stage(*tl[k_])

        def proj_fm(wt, rw, src, r_src, NT, evac, k_n=8, only=None):
            pend = []
            for ti, (c0, n) in enumerate(tiles_of(NT)):
                if only is not None and ti != only:
                    continue
                bl = blocks_of(c0, n)
                bk, rbk = nb()

                def mm(e, bk=bk, c0=c0, n=n):
                    for kc in range(k_n):
                        ins = e.matmul(bk[:, 0:n], lhsT=wt[:, kc, :], rhs=src[:, kc, c0:c0 + n], start=(kc == 0), stop=(kc == k_n - 1))
                    return ins
                S.add("pe", mm, reads=[rw] + r_src(bl), writes=[rbk])
                if evac is not None:
                    evac(bk, rbk, c0, n)
                pend.append((bk, rbk, c0, n))
            return pend

        def resid_add(bk, rbk, oc, c0, n):
            bl = blocks_of(c0, n)
            S.add("dve", lambda e: e.tensor_tensor(out=XT[:, oc, c0:c0 + n], in0=bk[:, 0:n], in1=XT[:, oc, c0:c0 + n], op=ALU.add),
                  reads=[rbk] + [r_X[b] for b in bl], writes=[r_X[b] for b in bl])

        def ffn(NT, gcol):
            rmsnorm(NT, gcol)
            for kg, nch in enumerate(KG):
                for j0 in range(0, nch, 2):
                    grp, rg = w_next()
                    for (c0, n) in tiles_of(NT):
                        bl = blocks_of(c0, n)
                        for j in (j0, j0 + 1):
                            gt, ut = grp[:, 2 * (j % 2)], grp[:, 2 * (j % 2) + 1]
                            b1, rb1 = nb()
                            b2, rb2 = nb()

                            def mm(e, b1=b1, b2=b2, gt=gt, ut=ut, c0=c0, n=n):
                                for kc in range(8):
                                    e.matmul(b1[:, 0:n], lhsT=gt[:, kc, :], rhs=hT[:, kc, c0:c0 + n], start=(kc == 0), stop=(kc == 7))
                                for kc in range(8):
                                    ins = e.matmul(b2[:, 0:n], lhsT=ut[:, kc, :], rhs=hT[:, kc, c0:c0 + n], start=(kc == 0), stop=(kc == 7))
                                return ins
                            S.add("pe", mm, reads=[rg] + rh(bl), writes=[rb1, rb2])
                            i = rot("sgt")
                            S.add("act", lambda e, b1=b1, i=i, n=n: e.activation(out=sgt[:, i, 0:n], in_=b1[:, 0:n], func=AF.Silu), reads=[rb1], writes=[r_sgt[i]])
                            act_uses("silu")
                            S.add("dve", lambda e, b2=b2, i=i, n=n, j=j, c0=c0: e.tensor_tensor(out=aT[:, j, c0:c0 + n], in0=b2[:, 0:n], in1=sgt[:, i, 0:n], op=ALU.mult),
                                  reads=[rb2, r_sgt[i]], writes=[(r_a if j < 4 else r_u)[b] for b in bl])
                for oc0 in range(0, 8, 4):
                    grp, rg = w_next()
                    for (c0, n) in tiles_of(NT):
                        bl = blocks_of(c0, n)
                        for oc in range(oc0, oc0 + 4):
                            wt = grp[:, oc % 4]
                            bk, rbk = nb()

                            def mm(e, bk=bk, wt=wt, c0=c0, n=n, nch=nch):
                                for kc in range(nch):
                                    ins = e.matmul(bk[:, 0:n], lhsT=wt[:, kc, :], rhs=aT[:, kc, c0:c0 + n], start=(kc == 0), stop=(kc == nch - 1))
                                return ins
                            S.add("pe", mm, reads=[rg] + [r_a[b] for b in bl] + [r_u[b] for b in bl], writes=[rbk])
                            resid_add(bk, rbk, oc, c0, n)

        def wout(NT):
            for oc0 in range(0, 8, 4):
                grp, rg = w_next()
                for (c0, n) in tiles_of(NT):
                    bl = blocks_of(c0, n)
                    for oc in range(oc0, oc0 + 4):
                        wt = grp[:, oc % 4]
                        bk, rbk = nb()

                        def mm(e, bk=bk, wt=wt, c0=c0, n=n):
                            for kc in range(8):
                                ins = e.matmul(bk[:, 0:n], lhsT=wt[:, kc, :], rhs=aT[:, kc, c0:c0 + n], start=(kc == 0), stop=(kc == 7))
                            return ins
                        S.add("pe", mm, reads=[rg] + [r_a[b] for b in bl] + [r_u[b] for b in bl], writes=[rbk])
                        resid_add(bk, rbk, oc, c0, n)

        def load_x(st, NT):
            if st > 0:
                barrier()
            nblk = NBLK + (1 if st == 0 else 0)
            for blk in range(nblk):
                rows = 128 if blk < NBLK else NS
                r0 = st * NTP + blk * 128 if blk < NBLK else 2 * NTP
                c0 = blk * 128
                i = rot("xtok")
                S.add("sp", lambda e, i=i, r0=r0, rows=rows: e.dma_start(out=xtok[0:rows, i, :], in_=xin[r0:r0 + rows, :]),
                      writes=[r_xtok[i]], dma_owner=r_xtok[i])
                for hh in range(2):
                    bk, rbk = nb()

                    def tr(e, bk=bk, i=i, hh=hh, rows=rows):
                        for c in range(4):
                            ins = e.transpose(bk[:, c * 128:c * 128 + rows], xtok[0:rows, i, (4 * hh + c) * 128:(4 * hh + c + 1) * 128], identF[0:rows, 0:rows])
                        return ins
                    S.add("pe", tr, reads=[r_xtok[i], r_cc], writes=[rbk])
                    src = lambda bk=bk, rows=rows: bk[:].rearrange("p (c n) -> p c n", c=4)[:, :, 0:rows]
                    if hh == 0:
                        S.add("act", lambda e, src=src, c0=c0, rows=rows: e.copy(out=XT[:, 0:4, c0:c0 + rows], in_=src()), reads=[rbk], writes=[r_X[blk]])
                    else:
                        S.add("dve", lambda e, src=src, c0=c0, rows=rows: e.tensor_copy(out=XT[:, 4:8, c0:c0 + rows], in_=src()), reads=[rbk], writes=[r_X[blk]])

        def store_y(st, NT):
            barrier()
            blks = list(range(2, NBLK)) if st == 0 else list(range(NBLK))
            if st == 0:
                blks.append(NBLK)
            for blk in blks:
                rows = 128 if blk < NBLK else NS
                c0 = blk * 128
                i = rot("xtok")
                for hh in range(2):
                    bk, rbk = nb()

                    def tr(e, bk=bk, hh=hh, rows=rows, c0=c0):
                        for c in range(4):
                            ins = e.transpose(bk[0:rows, c * 128:(c + 1) * 128], XT[:, 4 * hh + c, c0:c0 + rows], identF[:, :])
                        return ins
                    S.add("pe", tr, reads=[r_X[blk], r_cc], writes=[rbk])
                    if hh == 0:
                        S.add("act", lambda e, bk=bk, i=i, rows=rows: e.copy(out=xtok[0:rows, i, 0:512], in_=bk[0:rows, :]), reads=[rbk], writes=[r_xtok[i]])
                    else:
                        S.add("dve", lambda e, bk=bk, i=i, rows=rows: e.tensor_copy(out=xtok[0:rows, i, 512:1024], in_=bk[0:rows, :]), reads=[rbk], writes=[r_xtok[i]])
                if blk < NBLK:
                    y0 = (blk - 2) * 128 if st == 0 else (7 + blk) * 128
                    dst = y_d[y0:y0 + 128, :]
                else:
                    dst = ys_d
                out_ops.append(S.add("sp", lambda e, i=i, rows=rows, dst=dst: e.dma_start(out=dst, in_=xtok[0:rows, i, :]), reads=[r_xtok[i]], dma_owner=r_xtok[i]))

        def l0_mixer(st, NT):
            barrier()
            cf, cb = Carver(arF, ARF), Carver(arB, ARB)
            knF = cf.get(144); r_knF = arena_res("knF")
            rq = [cf.get(512) for _ in range(2)]; r_rq = [arena_res("rq%d" % i) for i in range(2)]
            rD = cf.get(512); r_rD = arena_res("rD")
            gl = [cf.get(512) for _ in range(2)]; r_gl = [arena_res("gl%d" % i) for i in range(2)]
            xh = [cf.get(512) for _ in range(2)]; r_xh = [arena_res("xh%d" % i) for i in range(2)]
            vln = cf.get(512); r_vln = arena_res("vln")
            stat = [cf.get(16) for _ in range(2)]; r_stat = [arena_res("stat%d" % i) for i in range(2)]
            Kall = cf.get(2048).rearrange("p (n f) -> p n f", n=NS); r_Kall = R("Kall", dma=True); allres.append(r_Kall)
            Vall = cf.get(2048).rearrange("p (n f) -> p n f", n=NS); r_Vall = R("Vall", dma=True); allres.append(r_Vall)
            vtokS = cf.get(128); r_vtokS = arena_res("vtokS")
            ktokS = cf.get(128); r_ktokS = arena_res("ktokS")
            vTs = cf.get(64).rearrange("p (c n) -> p c n", c=4); r_vTs = arena_res("vTs")
            attS = cf.get(128); r_attS = arena_res("attS")
            PT1 = cf.get(1024).bitcast(BF16).rearrange("p (i n) -> p i n", i=4)
            sqb = [cb.get(512) for _ in range(2)]; r_sqb = [arena_res("sqb%d" % i) for i in range(2)]
            PT0 = cb.get(2048).rearrange("p (i n) -> p i n", i=4)
            PTb = [PT0, PT1]
            r_PT = [[arena_res("PT%d_%d" % (b, i)) for i in range(4)] for b in range(2)]
            vbb = [cb.get(512) for _ in range(2)]; r_vbb = [arena_res("vbb%d" % i) for i in range(2)]
            KTs = cb.get(2048).rearrange("p (n f) -> p n f", n=NS); r_KTs = arena_res("KTs")
            Vbs = cb.get(2048).rearrange("p (n f) -> p n f", n=NS); r_Vbs = arena_res("Vbs")
            Qpad = cb.get(128).rearrange("p (n h) -> p n h", n=NS); r_Qpad = arena_res("Qpad")
            PTs = cb.get(128).rearrange("p (n h) -> p n h", n=NS); r_PTs = arena_res("PTs")
            qcnt = [0]
            last = NBLK - 1
            blk0 = 1 if st == 0 else 0

            if st == 0:
                S.add("sp", lambda e: e.dma_start(out=Kall[0:127], in_=ck_d.rearrange("n j f -> j n f")), writes=[r_Kall], dma_owner=r_Kall)
                S.add("sp", lambda e: e.dma_start(out=Vall[0:127], in_=cv_d.rearrange("n j f -> j n f")), writes=[r_Vall], dma_owner=r_Vall)
            else:
                S.add("dve", lambda e: e.tensor_copy(out=knT[:, 0, :], in_=knT[:, NBLK, :]), reads=[r_kn[last]], writes=[r_kn[9]])
                S.add("act", lambda e: e.copy(out=Vb[:, 0, :], in_=Vb[:, NBLK, :]), reads=[r_vb[last]], writes=[r_vb[9]])

            skipB0[0] = False
            rmsnorm(NT, PP["g32_0"])

            qk_pend = []

            def qk_flush():
                while qk_pend:
                    qk_pend.pop(0)()

            def qk_evac(dst_fn):
                def ev(bk, rbk, c0, n):
                    i = qcnt[0] % 2
                    qcnt[0] += 1
                    S.add("act", lambda e: e.activation(out=sqb[i][:, 0:n], in_=bk[:, 0:n], func=AF.Square), reads=[rbk], writes=[r_sqb[i]])

                    def stage2():
                        b2, rb2 = nb()
                        S.add("pe", lambda e: e.matmul(b2[:, 0:n], lhsT=blkB[:], rhs=sqb[i][:, 0:n], start=True, stop=True), reads=[r_sqb[i], r_cc], writes=[rb2])
                        rsqrt_act(rq[i][:, 0:n], b2[:, 0:n], epsc[:, 1:2], [rb2], r_rq[i])
                        dst_fn(bk, rbk, c0, n, i)
                    if qk_pend:
                        qk_pend.pop(0)()
                    qk_pend.append(stage2)
                return ev

            grp, rg = w_next()
            skipB0[0] = (128 if st == 0 else 0)
            qfns = []
            for c in range(4):
                def dst_q(bk, rbk, c0, n, i, c=c):
                    bl = blocks_of(c0, n)
                    S.add("dve", lambda e: e.scalar_tensor_tensor(out=aT[:, c, c0:c0 + n], in0=bk[:, 0:n], scalar=par[:, PC["qn"]:PC["qn"] + 1], in1=rq[i][:, 0:n], op0=ALU.mult, op1=ALU.mult),
                          reads=[rbk, r_rq[i], r_c], writes=[r_a[b] for b in bl])
                qfns.append((grp[:, c], qk_evac(dst_q)))
            for ti in range(3):
                for wt_, ev_ in qfns:
                    proj_fm(wt_, rg, hT, rh, NT, ev_, only=ti)
            grp, rg = w_next()
            skipB0[0] = False

            def dst_k(bk, rbk, c0, n, i):
                bl = blocks_of(c0, n)
                if c0 < 1024:
                    S.add("dve", lambda e: e.scalar_tensor_tensor(out=knT[:, 1 + c0 // 128:1 + (c0 + n) // 128, :].rearrange("p b n -> p (b n)"), in0=bk[:, 0:n], scalar=pp[:, 57:58], in1=rq[i][:, 0:n], op0=ALU.mult, op1=ALU.mult),
                          reads=[rbk, r_rq[i], r_pp], writes=[r_kn[b] for b in bl])
                else:
                    S.add("dve", lambda e: e.scalar_tensor_tensor(out=knF[:, 0:n], in0=bk[:, 0:n], scalar=pp[:, 57:58], in1=rq[i][:, 0:n], op0=ALU.mult, op1=ALU.mult),
                          reads=[rbk, r_rq[i], r_pp], writes=[r_knF])
                    S.add("act", lambda e: e.copy(out=knT[:, NBLK, :], in_=knF[:, 0:128]), reads=[r_knF], writes=[r_kn[last]])
            proj_fm(grp[:, 0], rg, hT, rh, NT, qk_evac(dst_k))
            qk_flush()

            vt = grp[:, 1]
            nblk = NBLK + (1 if st == 0 else 0)
            for b0 in range(0, nblk, 4):
                bs = list(range(b0, min(b0 + 4, nblk)))
                bk, rbk = nb()

                def mm(e, bk=bk, bs=bs):
                    for j, blk in enumerate(bs):
                        rows = 128 if blk < NBLK else NS
                        for kc in range(8):
                            ins = e.matmul(bk[0:rows, j * 128:(j + 1) * 128], lhsT=hT[:, kc, blk * 128:blk * 128 + rows], rhs=vt[:, kc, :], start=(kc == 0), stop=(kc == 7))
                    return ins
                S.add("pe", mm, reads=[rg] + rh(bs), writes=[rbk])
                for j, blk in enumerate(bs):
                    if blk < NBLK:
                        S.add("act", lambda e, bk=bk, j=j, blk=blk: e.copy(out=Vb[:, 1 + blk, :], in_=bk[:, j * 128:(j + 1) * 128]), reads=[rbk], writes=[r_vb[blk]])
                        if st == 1 and blk == last:
                            S.add("dve", lambda e, bk=bk, j=j: e.tensor_copy(out=vln[:, 0:128], in_=bk[:, j * 128:(j + 1) * 128]), reads=[rbk], writes=[r_vln])
                            out_ops.append(S.add("sp", lambda e: e.dma_start(out=nv_d, in_=vln[:, 0:128]), reads=[r_vln], dma_owner=r_onv))
                    else:
                        S.add("act", lambda e, bk=bk, j=j: e.copy(out=vtokS[0:NS, :], in_=bk[0:NS, j * 128:(j + 1) * 128]), reads=[rbk], writes=[r_vtokS])
                        S.add("sp", lambda e: e.dma_start(out=Vall[127:128], in_=vtokS[0:NS, :]), reads=[r_vtokS, r_Vall], writes=[r_Vall], dma_owner=r_Vall)

            if st == 1:
                bk, rbk = nb()
                S.add("pe", lambda e, bk=bk: e.transpose(bk[:, 0:128], knF[:, 0:128], identF[:]), reads=[r_knF, r_cc], writes=[rbk])
                S.add("dve", lambda e, bk=bk: e.tensor_copy(out=xh[0][:, 0:128], in_=bk[:, 0:128]), reads=[rbk], writes=[r_xh[0]])
                out_ops.append(S.add("sp", lambda e: e.dma_start(out=nk_d, in_=xh[0][:, 0:128]), reads=[r_xh[0]], dma_owner=r_onk))
            else:
                bk, rbk = nb()
                S.add("pe", lambda e, bk=bk: e.transpose(bk[0:NS, 0:128], knF[:, 128:128 + NS], identF[:]), reads=[r_knF, r_cc], writes=[rbk])
                S.add("dve", lambda e, bk=bk: e.tensor_copy(out=ktokS[0:NS, :], in_=bk[0:NS, 0:128]), reads=[rbk], writes=[r_ktokS])
                S.add("sp", lambda e: e.dma_start(out=Kall[127:128], in_=ktokS[0:NS, :]), reads=[r_ktokS, r_Kall], writes=[r_Kall], dma_owner=r_Kall)

            grp, rg = w_next()
            skipB0[0] = (128 if st == 0 else 0)
            ufns = []
            for c in range(4):
                def ev_u(bk, rbk, c0, n, c=c):
                    bl = blocks_of(c0, n)
                    S.add("act", lambda e: e.activation(out=aT[:, 4 + c, c0:c0 + n], in_=bk[:, 0:n], func=AF.Gelu), reads=[rbk], writes=[r_u[b] for b in bl])
                    act_uses("gelu")
                ufns.append((grp[:, c], ev_u))
            for ti in range(3):
                for wt_, ev_ in ufns:
                    proj_fm(wt_, rg, hT, rh, NT, ev_, only=ti)
            skipB0[0] = False

            def attn_s1(blk):
                has_prev = not (st == 0 and blk == 0)
                special = (st == 0 and blk == 2)
                kbs = ([0] if has_prev else []) + [1]
                c0 = blk * 128
                pb = blk % 2
                for g in range(2):
                    for kb in kbs:
                        i = g * 2 + kb
                        slot = blk + kb
                        rk = r_kn[blk - 1 if (kb == 0 and blk > 0) else (9 if kb == 0 else blk)]
                        bk, rbk = nb()
                        T = Tb[0] if kb == 0 else Tb[1]

                        def mm(e, bk=bk, g=g, slot=slot, T=T):
                            o = bk[:].rearrange("p (c n) -> p c n", c=4)
                            e.matmul(o, lhsT=knT[g * 64:(g + 1) * 64, slot, :], rhs=aT[g * 64:(g + 1) * 64, 0:4, c0:c0 + 128], start=True, stop=False)
                            return e.matmul(o, lhsT=identB[:], rhs=T[:, 4 * g:4 * g + 4, :], start=False, stop=True)
                        S.add("pe", mm, reads=[rk, r_a[blk], r_tab2, r_cc], writes=[rbk])
                        act6()
                        if special and kb == 0:
                            S.add("act", lambda e, bk=bk, i=i: e.activation(out=PTb[pb][:, i, :], in_=bk[:], func=AF.Exp, bias=hmc[:, 0:1]), reads=[rbk, r_c], writes=[r_PT[pb][i]])
                        else:
                            S.add("act", lambda e, bk=bk, i=i: e.activation(out=PTb[pb][:, i, :], in_=bk[:], func=AF.Exp), reads=[rbk], writes=[r_PT[pb][i]])

            def attn_s2(blk):
                has_prev = not (st == 0 and blk == 0)
                kbs = ([0] if has_prev else []) + [1]
                c0 = blk * 128
                pb = blk % 2
                PT = PTb[pb]
                bo, rbo = nb()
                bd, rbd = nb()

                def mm(e, bo=bo, bd=bd):
                    for g in range(2):
                        for j, kb in enumerate(kbs):
                            slot = blk + kb
                            e.matmul(bo[g * 64:(g + 1) * 64, :], lhsT=Vb[:, slot, g * 64:(g + 1) * 64], rhs=PT[:, g * 2 + kb, :], start=(j == 0), stop=(j == len(kbs) - 1))
                    for g in range(2):
                        for j, kb in enumerate(kbs):
                            e.matmul(bd[g * 64:(g + 1) * 64, :], lhsT=onesB[:, 0:64], rhs=PT[:, g * 2 + kb, :], start=(j == 0), stop=False)
                        ins = e.matmul(bd[g * 64:(g + 1) * 64, :].rearrange("p (c n) -> p c n", c=4), lhsT=onesB[0:1, 0:64], rhs=esrow[0:1, 4 * g:4 * g + 4, :], start=False, stop=True)
                    return ins
                rv = [r_vb[blk]] + ([r_vb[blk - 1 if blk > 0 else 9]] if has_prev else [])
                S.add("pe", mm, reads=rv + [r_PT[pb][g * 2 + kb] for g in range(2) for kb in kbs] + [r_sk, r_cc], writes=[rbo, rbd])
                S.add("dve", lambda e, bd=bd: e.reciprocal(out=rD[:], in_=bd[:]), reads=[rbd], writes=[r_rD])
                S.add("dve", lambda e, bo=bo: e.tensor_tensor(out=aT[:, 0:4, c0:c0 + 128], in0=bo[:].rearrange("p (c n) -> p c n", c=4), in1=rD[:].rearrange("p (c n) -> p c n", c=4), op=ALU.mult),
                      reads=[rbo, r_rD], writes=[r_a[blk]])

            grp, rg = w_next()

            def gm_s1(blk):
                rows = 128 if blk < NBLK else NS
                c0 = blk * 128
                q = blk % 2
                bk, rbk = nb()

                def mm(e, bk=bk, rows=rows, c0=c0):
                    for kc in range(8):
                        ins = e.matmul(bk[0:rows, :].rearrange("p (c n) -> p c n", c=4), lhsT=hT[:, kc, c0:c0 + rows], rhs=grp[:, :, kc, :], start=(kc == 0), stop=(kc == 7))
                    return ins
                S.add("pe", mm, reads=[rg] + rh([blk]), writes=[rbk])
                S.add("dve", lambda e: e.memset(stat[q][:, 0:2], 0.0), writes=[r_stat[q]])
                S.add("act", lambda e, bk=bk: e.activation(out=gl[q][0:rows, :], in_=bk[0:rows, :], func=AF.Gelu, accum_out=stat[q][0:rows, 0:1]), reads=[rbk], writes=[r_gl[q], r_stat[q]])
                act_uses("gelu")
                S.add("act", lambda e: e.activation(out=xh[q][0:rows, :], in_=gl[q][0:rows, :], func=AF.Square, accum_out=stat[q][0:rows, 1:2]), reads=[r_gl[q]], writes=[r_xh[q], r_stat[q]])
                st_ = stat[q]
                S.seq("dve", [
                    lambda e: e.tensor_scalar(out=st_[0:rows, 2:4], in0=st_[0:rows, 0:2], scalar1=1.0 / 512.0, scalar2=None, op0=ALU.mult),
                    lambda e: e.tensor_tensor(out=st_[0:rows, 4:5], in0=st_[0:rows, 2:3], in1=st_[0:rows, 2:3], op=ALU.mult),
                    lambda e: e.tensor_tensor(out=st_[0:rows, 5:6], in0=st_[0:rows, 3:4], in1=st_[0:rows, 4:5], op=ALU.subtract),
                ], writes=[r_stat[q]])
                rsqrt_act(st_[0:rows, 6:7], st_[0:rows, 5:6], epsc[0:rows, 2:3], [], r_stat[q])
                S.seq("dve", [
                    lambda e: e.scalar_tensor_tensor(out=st_[0:rows, 7:8], in0=st_[0:rows, 2:3], scalar=-1.0, in1=st_[0:rows, 6:7], op0=ALU.mult, op1=ALU.mult),
                ], writes=[r_stat[q]])
                S.add("act", lambda e: e.activation(out=xh[q][0:rows, :], in_=gl[q][0:rows, :], func=AF.Identity, scale=st_[0:rows, 6:7], bias=st_[0:rows, 7:8]),
                      reads=[r_gl[q], r_stat[q]], writes=[r_xh[q]])
                S.add("dve", lambda e: e.tensor_tensor(out=xh[q][0:rows, :], in0=xh[q][0:rows, :], in1=bcast[0:rows, 0:512], op=ALU.mult), reads=[r_c], writes=[r_xh[q]])
                need_f32 = (blk == NBLK) or (st == 1 and blk == last)
                if need_f32:
                    S.add("dve", lambda e: e.tensor_tensor(out=vln[0:rows, :], in0=xh[q][0:rows, :], in1=bcast[0:rows, 512:1024], op=ALU.add), reads=[r_xh[q], r_c], writes=[r_vln])
                    dst = gvs_d if blk == NBLK else gv_d
                    out_ops.append(S.add("sp", lambda e: e.dma_start(out=dst, in_=vln[0:rows, :]), reads=[r_vln], dma_owner=(r_ogvs if blk == NBLK else r_ogv)))
                if blk < NBLK:
                    if need_f32:
                        S.add("act", lambda e: e.copy(out=vbb[q][:], in_=vln[:]), reads=[r_vln], writes=[r_vbb[q]])
                    else:
                        S.add("dve", lambda e: e.tensor_tensor(out=vbb[q][:], in0=xh[q][:], in1=bcast[:, 512:1024], op=ALU.add), reads=[r_xh[q], r_c], writes=[r_vbb[q]])

            def gm_s2(blk):
                c0 = blk * 128
                q = blk % 2
                if blk < NBLK:
                    bs_, rbs = nb()

                    def mmg(e, bs_=bs_):
                        for g in range(8):
                            o = bs_[(g % 2) * 64:(g % 2) * 64 + 64, (g // 2) * 128:(g // 2 + 1) * 128]
                            ins = e.matmul(o, lhsT=vbb[q][:, g * 64:(g + 1) * 64], rhs=WsT[:, g, :], start=True, stop=True)
                        return ins
                    S.add("pe", mmg, reads=[r_vbb[q], r_c2, r_cc], writes=[rbs])
                    S.add("dve", lambda e, bs_=bs_: e.tensor_tensor(out=xh[q][:], in0=bs_[:], in1=bsG[:], op=ALU.add), reads=[rbs, r_c], writes=[r_xh[q]])
                    S.add("dve", lambda e: e.tensor_tensor(out=aT[:, 4:8, c0:c0 + 128], in0=xh[q][:].rearrange("p (c n) -> p c n", c=4), in1=aT[:, 4:8, c0:c0 + 128], op=ALU.mult),
                          reads=[r_xh[q], r_u[blk]], writes=[r_u[blk]])
                else:
                    bt, rbt = nb()

                    def trs(e, bt=bt):
                        for c in range(4):
                            ins = e.transpose(bt[:, c * NS:(c + 1) * NS], vln[0:NS, c * 128:(c + 1) * 128], identF[0:NS, 0:NS])
                        return ins
                    S.add("pe", trs, reads=[r_vln, r_cc], writes=[rbt])

                    def sops(e, bt=bt):
                        for c in range(4):
                            ins = e.tensor_scalar(out=vTs[:, c, :], in0=bt[:, c * NS:(c + 1) * NS], scalar1=par[:, PC["w00"] + c:PC["w00"] + c + 1], scalar2=par[:, PC["bs0"] + c:PC["bs0"] + c + 1], op0=ALU.mult, op1=ALU.add)
                        return ins
                    S.add("dve", sops, reads=[rbt, r_c], writes=[r_vTs])
                    S.add("act", lambda e: e.copy(out=rD[:, 0:64].rearrange("p (c n) -> p c n", c=4), in_=aT[:, 4:8, NTP:NTP + NS]), reads=[r_u[9]], writes=[r_rD])
                    S.add("dve", lambda e: e.tensor_tensor(out=aT[:, 4:8, NTP:NTP + NS], in0=vTs, in1=rD[:, 0:64].rearrange("p (c n) -> p c n", c=4), op=ALU.mult), reads=[r_vTs, r_rD], writes=[r_u[9]])

            blks = list(range(blk0, NBLK))
            gblks = blks + ([NBLK] if st == 0 else [])
            attn_s1(blks[0])
            gm_s1(gblks[0])
            for idx in range(len(gblks)):
                if idx + 1 < len(blks):
                    attn_s1(blks[idx + 1])
                if idx + 1 < len(gblks):
                    gm_s1(gblks[idx + 1])
                if idx < len(blks):
                    attn_s2(blks[idx])
                gm_s2(gblks[idx])

            if st == 0:
                out_ops.append(S.add("sp", lambda e: e.dma_start(out=nks_d.rearrange("n j f -> j n f"), in_=Kall[:]), reads=[r_Kall], dma_owner=r_Kall))
                out_ops.append(S.add("sp", lambda e: e.dma_start(out=nvs_d.rearrange("n j f -> j n f"), in_=Vall[:]), reads=[r_Vall], dma_owner=r_Vall))
                S.add("act", lambda e: e.copy(out=Vbs[:], in_=Vall[:]), reads=[r_Vall], writes=[r_Vbs])
                for n0 in range(0, NS, 4):
                    bk, rbk = nb()

                    def tr(e, bk=bk, n0=n0):
                        for j in range(4):
                            ins = e.transpose(bk[:, j * 128:(j + 1) * 128], Kall[:, n0 + j, :], identF[:])
                        return ins
                    S.add("pe", tr, reads=[r_Kall, r_cc], writes=[rbk])
                    S.add("dve", lambda e, bk=bk, n0=n0: e.tensor_copy(out=KTs[:, n0:n0 + 4, :], in_=bk[:].rearrange("p (c n) -> p c n", c=4)), reads=[rbk], writes=[r_KTs])
                S.add("dve", lambda e: e.memset(Qpad[:], 0.0), writes=[r_Qpad])
                S.add("act", lambda e: e.copy(out=Qpad[0:64, :, 0:4].rearrange("p n c -> p c n"), in_=aT[0:64, 0:4, NTP:NTP + NS]), reads=[r_a[9]], writes=[r_Qpad])
                S.add("dve", lambda e: e.tensor_copy(out=Qpad[64:128, :, 4:8].rearrange("p n c -> p c n"), in_=aT[64:128, 0:4, NTP:NTP + NS]), reads=[r_a[9]], writes=[r_Qpad])
                bk, rbk = nb()

                def mms(e, bk=bk):
                    for n in range(NS):
                        ins = e.matmul(bk[:, n * 8:(n + 1) * 8], lhsT=KTs[:, n, :], rhs=Qpad[:, n, :], start=True, stop=True)
                    return ins
                S.add("pe", mms, reads=[r_KTs, r_Qpad], writes=[rbk])
                S.add("dve", lambda e, bk=bk: e.tensor_tensor(out=attS.rearrange("p (n h) -> p n h", n=NS), in0=bk[:, 0:128].rearrange("p (n h) -> p n h", n=NS),
                                                             in1=biasS[:].unsqueeze(1).broadcast_to([128, NS, 8]), op=ALU.add), reads=[rbk, r_tab2], writes=[r_attS])
                act6()
                S.add("act", lambda e: e.activation(out=PTs[:], in_=attS.rearrange("p (n h) -> p n h", n=NS), func=AF.Exp), reads=[r_attS], writes=[r_PTs])
                bo, rbo = nb()
                bd, rbd = nb()

                def mmo(e, bo=bo, bd=bd):
                    for n in range(NS):
                        for g in range(2):
                            e.matmul(bo[g * 64:(g + 1) * 64, n * 4:(n + 1) * 4], lhsT=Vbs[:, n, g * 64:(g + 1) * 64], rhs=PTs[:, n, 4 * g:4 * g + 4], start=True, stop=True)
                            e.matmul(bd[g * 64:(g + 1) * 64, n * 4:(n + 1) * 4], lhsT=onesB[:, 0:64], rhs=PTs[:, n, 4 * g:4 * g + 4], start=True, stop=False)
                            ins = e.matmul(bd[g * 64:(g + 1) * 64, n * 4:(n + 1) * 4], lhsT=onesB[0:1, 0:64], rhs=es8[0:1, 4 * g:4 * g + 4], start=False, stop=True)
                    return ins
                S.add("pe", mmo, reads=[r_Vbs, r_PTs, r_sk, r_cc], writes=[rbo, rbd])
                S.add("dve", lambda e, bd=bd: e.reciprocal(out=rD[:, 0:64], in_=bd[:, 0:64]), reads=[rbd], writes=[r_rD])
                S.add("dve", lambda e, bo=bo: e.tensor_tensor(out=aT[:, 0:4, NTP:NTP + NS], in0=bo[:, 0:64].rearrange("p (n c) -> p c n", c=4), in1=rD[:, 0:64].rearrange("p (n c) -> p c n", c=4), op=ALU.mult),
                      reads=[rbo, r_rD], writes=[r_a[9]])

            skipB0[0] = (128 if st == 0 else 0)
            wout(NT)

        def l1_mixer(st, NT):
            barrier()
            cf, cb = Carver(arF, ARF), Carver(arB, ARB)
            qT = cf.get(NTA); r_qT = arena_res("qT")
            sg = cf.get(NTA); r_sg = arena_res("sg")
            km = cf.get(NTA); r_km = arena_res("km")
            Bc = cf.get(NTP); r_B = arena_res("B")
            Ei = cf.get(NTP); r_Ei = arena_res("Ei")
            oT = cf.get(NTA); r_oT = arena_res("oT")
            oscr = cf.get(NTA); r_oscr = arena_res("oscr")
            Sin = cf.get(2048).rearrange("p (n e) -> p n e", n=NS); r_Sin = R("Sin", dma=True); allres.append(r_Sin)
            Sout = cf.get(2048).rearrange("p (n e) -> p n e", n=NS); r_Sout = R("Sout", dma=True); allres.append(r_Sout)
            smb = [cf.get(64) for _ in range(2)]; r_smb = [arena_res("sm%d" % i) for i in range(2)]
            ktokSb = [cf.get(128) for _ in range(2)]; r_ktokSb = [arena_res("ktokS%d" % i) for i in range(2)]
            rrs = [cf.get(128) for _ in range(2)]; r_rrs = [arena_res("rrs%d" % i) for i in range(2)]
            fSb = [cf.get(NS) for _ in range(2)]; r_fSb = [arena_res("fS%d" % i) for i in range(2)]
            qSb = [cf.get(NS) for _ in range(2)]; r_qSb = [arena_res("qS%d" % i) for i in range(2)]
            qtbb = [cb.get(NTP) for _ in range(2)]; r_qtbb = [arena_res("qtb%d" % i) for i in range(2)]
            ktbb = [cb.get(NTP) for _ in range(2)]; r_ktbb = [arena_res("ktb%d" % i) for i in range(2)]
            ktok = cb.get(NTP).rearrange("p (b n) -> p b n", b=NBLK); r_ktok = arena_res("ktok")
            vtokb = [cb.get(NTP).rearrange("p (b n) -> p b n", b=NBLK) for _ in range(2)]; r_vtokb = [arena_res("vtok%d" % i) for i in range(2)]
            gatb = [cb.get(NTA) for _ in range(2)]; r_gatb = [arena_res("gat%d" % i) for i in range(2)]
            sq = cb.get(NTA); r_sq = arena_res("sq")
            Am = [cb.get(128) for _ in range(2)]; r_Am = [arena_res("Am%d" % i) for i in range(2)]
            S0b2 = [cb.get(128) for _ in range(2)]; r_S0b2 = [arena_res("S0b%d" % i) for i in range(2)]
            kmask = [cb.get(128) for _ in range(2)]; r_kmask = [arena_res("kmask%d" % i) for i in range(2)]
            vtokS = cb.get(128); r_vtokS = arena_res("vtokS1")
            ck0 = 1 if st == 0 else 0
            B3 = Bc.rearrange("p (b n) -> p b n", b=NBLK)
            E3 = Ei.rearrange("p (b n) -> p b n", b=NBLK)
            sc = slice(NTP, NTP + NS)
            c_lo = 128 if st == 0 else 0

            skipB0[0] = False
            rmsnorm(NT, PP["g32_1"])
            hgrp = {}
            pending_on = []

            def A_steps(h):
                p = h % 2
                sm = smb[p]
                steps = []

                def f_tile(ti):
                    def fn():
                        skipB0[0] = False
                        if ti == 0:
                            hgrp[h] = w_next()
                        grp, rg = hgrp[h]
                        for (bk, rbk, c0, n) in proj_fm(grp[:, 1], rg, hT, rh, NT, None, only=ti):
                            S.add("act", lambda e, bk=bk, c0=c0, n=n: e.activation(out=sg[:, c0:c0 + n], in_=bk[:, 0:n], func=AF.Tanh, scale=0.5), reads=[rbk], writes=[r_sg])
                    return fn

                def g_tile(ti):
                    def fn():
                        skipB0[0] = False
                        grp, rg = hgrp[h]
                        for (bk, rbk, c0, n) in proj_fm(grp[:, 3], rg, hT, rh, NT, None, only=ti):
                            S.add("act", lambda e, bk=bk, c0=c0, n=n: e.activation(out=gatb[p][:, c0:c0 + n], in_=bk[:, 0:n], func=AF.Tanh, scale=0.5), reads=[rbk], writes=[r_gatb[p]])
                    return fn

                def q_tile(ti):
                    def fn():
                        skipB0[0] = False
                        grp, rg = hgrp[h]
                        for (bk, rbk, c0, n) in proj_fm(grp[:, 0], rg, hT, rh, NT, None, only=ti):
                            S.add("act", lambda e, bk=bk, c0=c0, n=n: e.copy(out=qT[:, c0:c0 + n], in_=bk[:, 0:n]), reads=[rbk], writes=[r_qT])
                        if ti == 2 and st == 0:
                            S.add("dve", lambda e: e.tensor_copy(out=qSb[p][:], in_=qT[:, sc]), reads=[r_qT], writes=[r_qSb[p]])
                    return fn

                def v_tile(b0):
                    def fn():
                        grp, rg = hgrp[h]
                        vt = grp[:, 2]
                        bs = list(range(b0, min(b0 + 4, NBLK)))
                        bv, rbv = nb()

                        def mmv(e, bv=bv, bs=bs, vt=vt):
                            for j, blk in enumerate(bs):
                                for kc in range(8):
                                    ins = e.matmul(bv[:, j * 128:(j + 1) * 128], lhsT=hT[:, kc, blk * 128:(blk + 1) * 128], rhs=vt[:, kc, :], start=(kc == 0), stop=(kc == 7))
                            return ins
                        S.add("pe", mmv, reads=[rg] + rh(bs), writes=[rbv])
                        S.add("act", lambda e, bv=bv, bs=bs, b0=b0: e.copy(out=vtokb[p][:, b0:b0 + len(bs), :], in_=bv[:, 0:128 * len(bs)].rearrange("p (c n) -> p c n", c=len(bs))),
                              reads=[rbv], writes=[r_vtokb[p]])
                    return fn

                def km_ln():
                    S.add("dve", lambda e: e.tensor_scalar(out=km[:, 0:NT], in0=sg[:, 0:NT], scalar1=1.0, scalar2=pp[:, PP["lbm1"] + h:PP["lbm1"] + h + 1], op0=ALU.subtract, op1=ALU.mult),
                          reads=[r_sg, r_pp], writes=[r_km])
                    S.add("act", lambda e: e.activation(out=sg[:, 0:NT], in_=sg[:, 0:NT], func=AF.Ln, scale=pp[:, PP["oml"] + h:PP["oml"] + h + 1], bias=pp[:, PP["lb"] + h:PP["lb"] + h + 1]),
                          reads=[r_pp], writes=[r_sg])

                def s4a():
                    if st == 0:
                        S.add("dve", lambda e: e.memset(Bc[:, 0:128], 0.0), writes=[r_B])
                    S.add("dve", lambda e: e.tensor_tensor_scan(out=Bc[:, c_lo:NTP], data0=onesF[:, 0:1].broadcast_to([128, NTP - c_lo]), data1=sg[:, c_lo:NTP], initial=0.0, op0=ALU.mult, op1=ALU.add),
                          reads=[r_sg, r_cc], writes=[r_B])

                def s4b():
                    def sm1(e):
                        e.tensor_copy(out=sm[:, 0:9], in_=B3[:, :, 63])
                        e.memset(sm[:, 9:10], 0.0)
                        e.tensor_copy(out=sm[:, 10:18], in_=B3[:, 0:8, 127])
                        return e.tensor_copy(out=sm[:, 18:27], in_=B3[:, :, 127])
                    S.add("dve", sm1, reads=[r_B], writes=[r_smb[p]])

                    def sm2(e):
                        e.tensor_tensor(out=sm[:, 27:36], in0=sm[:, 0:9], in1=sm[:, 9:18], op=ALU.subtract)
                        e.tensor_tensor(out=sm[:, 36:45], in0=sm[:, 18:27], in1=sm[:, 9:18], op=ALU.subtract)
                        e.tensor_tensor(out=sm[:, 45:54], in0=sm[:, 18:27], in1=sm[:, 0:9], op=ALU.subtract)
                        return e.tensor_tensor(out=B3, in0=B3, in1=sm[:, 0:9].unsqueeze(2).broadcast_to([128, NBLK, 128]), op=ALU.subtract)
                    S.add("dve", sm2, writes=[r_B, r_smb[p]])

                def s5():
                    S.add("act", lambda e: e.activation(out=Ei[:], in_=Bc[:], func=AF.Exp, scale=-1.0), reads=[r_B], writes=[r_Ei])
                    S.add("act", lambda e: e.activation(out=Bc[:], in_=Bc[:], func=AF.Exp), writes=[r_B])
                    S.add("act", lambda e: e.activation(out=sm[:, 27:54], in_=sm[:, 27:54], func=AF.Exp), writes=[r_smb[p]])
                    if st == 0:
                        S.add("act", lambda e: e.activation(out=fSb[p][:], in_=sg[:, sc], func=AF.Exp), reads=[r_sg], writes=[r_fSb[p]])

                def s6a():
                    S.add("dve", lambda e: e.tensor_tensor(out=Ei[:], in0=km[:, 0:NTP], in1=Ei[:], op=ALU.mult), reads=[r_km], writes=[r_Ei])

                def s6b():
                    S.add("dve", lambda e: e.tensor_tensor(out=qtbb[p][:], in0=qT[:, 0:NTP], in1=Bc[:], op=ALU.mult), reads=[r_qT, r_B], writes=[r_qtbb[p]])

                def s7():
                    S.add("act", lambda e: e.copy(out=ktbb[p][:], in_=Ei[:]), reads=[r_Ei], writes=[r_ktbb[p]])
                    if st == 0:
                        bt, rbt = nb()
                        S.add("pe", lambda e, bt=bt: e.transpose(bt[0:NS, 0:128], km[:, sc], identF[:]), reads=[r_km, r_cc], writes=[rbt])
                        S.add("dve", lambda e, bt=bt: e.tensor_copy(out=ktokSb[p][0:NS, :], in_=bt[0:NS, 0:128]), reads=[rbt], writes=[r_ktokSb[p]])

                bundles = [[f_tile(0)], [f_tile(1)], [f_tile(2), km_ln, g_tile(0)], [s4a, g_tile(1), g_tile(2)], [s4b, q_tile(0)],
                           [s5, q_tile(1)], [s6a, q_tile(2), v_tile(0)], [s6b, s7, v_tile(4)], [v_tile(8)]]
                return bundles

            for bnd in A_steps(0):
                for f_ in bnd:
                    f_()
            def head_b1(h):
                p = h % 2
                sm, r_sm = smb[p], r_smb[p]
                qtb, r_qtb, ktb, r_ktb = qtbb[p], r_qtbb[p], ktbb[p], r_ktbb[p]
                gat, r_gat = gatb[p], r_gatb[p]
                grp, rg = hgrp[h]
                skipB0[0] = False
                if st == 0:
                    S.add("sp", lambda e, h=h: e.dma_start(out=Sin[:], in_=sin_d[:, h].rearrange("n d e -> d n e")), writes=[r_Sin], dma_owner=r_Sin)
                vt = grp[:, 2]
                vtok, r_vtok = vtokb[p], r_vtokb[p]
                if st == 0:
                    bv, rbv = nb()

                    def mmvs(e, bv=bv, vt=vt):
                        for kc in range(8):
                            ins = e.matmul(bv[0:NS, 0:128], lhsT=hT[:, kc, sc], rhs=vt[:, kc, :], start=(kc == 0), stop=(kc == 7))
                        return ins
                    S.add("pe", mmvs, reads=[rg] + rh([9]), writes=[rbv])
                    S.add("act", lambda e, bv=bv: e.copy(out=vtokS[0:NS, :], in_=bv[0:NS, 0:128]), reads=[rbv], writes=[r_vtokS])
                for b0 in range(0, NBLK, 4):
                    bs = list(range(b0, min(b0 + 4, NBLK)))
                    bk, rbk = nb()

                    def trk(e, bk=bk, bs=bs):
                        for j, blk in enumerate(bs):
                            ins = e.transpose(bk[:, j * 128:(j + 1) * 128], E3[:, blk, :], identF[:])
                        return ins
                    S.add("pe", trk, reads=[r_Ei, r_cc], writes=[rbk])
                    S.add("dve", lambda e, bk=bk, bs=bs, b0=b0: e.tensor_copy(out=ktok[:, b0:b0 + len(bs), :], in_=bk[:, 0:128 * len(bs)].rearrange("p (c n) -> p c n", c=len(bs))),
                          reads=[rbk], writes=[r_ktok])

            def run_head(h):
                p = h % 2
                sm, r_sm = smb[p], r_smb[p]
                qtb, r_qtb, ktb, r_ktb = qtbb[p], r_qtbb[p], ktbb[p], r_ktbb[p]
                gat, r_gat = gatb[p], r_gatb[p]
                grp, rg = hgrp[h]
                vtok, r_vtok = vtokb[p], r_vtokb[p]
                skipB0[0] = False
                nxt = A_steps(h + 1) if h < 7 else [[], []]
                for k_, f_ in enumerate(pending_on):
                    nxt[k_] = [f_] + nxt[k_]
                del pending_on[:]

                Sh = Sst[:, h, :]
                if st == 0:
                    S.add("dve", lambda e: e.memset(S0b2[ck0 % 2][:], 0.0), writes=[r_S0b2[ck0 % 2]])
                else:
                    S.add("dve", lambda e, Sh=Sh, sm=sm: e.tensor_scalar(out=S0b2[ck0 % 2][:], in0=Sh, scalar1=sm[:, 27:28], scalar2=None, op0=ALU.mult), reads=[r_S, r_sm], writes=[r_S0b2[ck0 % 2]])
                chunks = list(range(ck0, NBLK))

                def scan_pre(ck):
                    c0 = ck * 128
                    ia = ck % 2
                    ba, rba = nb()
                    S.add("pe", lambda e: e.matmul(ba[:, 0:128], lhsT=ktb[:, c0:c0 + 128], rhs=qtb[:, c0:c0 + 128], start=True, stop=True), reads=[r_ktb, r_qtb], writes=[rba])
                    S.add("dve", lambda e: e.tensor_tensor(out=Am[ia][:], in0=ba[:, 0:128], in1=maskT[:], op=ALU.mult), reads=[rba, r_cc], writes=[r_Am[ia]])
                    bkv, rbkv = nb()
                    S.add("pe", lambda e: e.matmul(bkv[:, 0:128], lhsT=ktok[:, ck, :], rhs=vtok[:, ck, :], start=True, stop=True), reads=[r_ktok, r_vtok], writes=[rbkv])
                    S.add("dve", lambda e: e.tensor_scalar(out=rrs[ia][:], in0=bkv[:, 0:128], scalar1=sm[:, 45 + ck:46 + ck], scalar2=None, op0=ALU.mult),
                          reads=[rbkv, r_sm], writes=[r_rrs[ia]])
                scan_pre(chunks[0])
                bo = rbo = None
                for idx, ck in enumerate(chunks):
                    c0 = ck * 128
                    ia = ck % 2
                    if idx + 1 < len(chunks):
                        scan_pre(chunks[idx + 1])
                    j = idx % 2
                    if j == 0:
                        bo, rbo = nb()

                    def mmo(e, bo=bo, j=j, ck=ck, c0=c0, ia=ia):
                        e.matmul(bo[:, j * 128:(j + 1) * 128], lhsT=vtok[:, ck, :], rhs=Am[ia][:], start=True, stop=False)
                        return e.matmul(bo[:, j * 128:(j + 1) * 128], lhsT=S0b2[ia][:], rhs=qtb[:, c0:c0 + 128], start=False, stop=True)
                    S.add("pe", mmo, reads=[r_vtok, r_Am[ia], r_S0b2[ia], r_qtb], writes=[rbo])
                    if j == 1 or idx == len(chunks) - 1:
                        cs = (ck - j) * 128
                        S.add("act", lambda e, bo=bo, cs=cs, j=j: e.copy(out=oT[:, cs:cs + 128 * (j + 1)], in_=bo[:, 0:128 * (j + 1)]), reads=[rbo], writes=[r_oT])
                    S.add("dve", lambda e, ck=ck, Sh=Sh, ia=ia, sm=sm: e.scalar_tensor_tensor(out=Sh, in0=Sh, scalar=sm[:, 36 + ck:37 + ck], in1=rrs[ia][:], op0=ALU.mult, op1=ALU.add),
                          reads=[r_rrs[ia], r_sm], writes=[r_S])
                    if idx + 1 < len(chunks):
                        S.add("dve", lambda e, ck=ck, Sh=Sh, ia=ia, sm=sm: e.tensor_scalar(out=S0b2[1 - ia][:], in0=Sh, scalar1=sm[:, 28 + ck:29 + ck], scalar2=None, op0=ALU.mult),
                              reads=[r_S, r_sm], writes=[r_S0b2[1 - ia]])
                    if nxt:
                        for f_ in nxt.pop(0):
                            f_()
                while nxt:
                    for f_ in nxt.pop(0):
                        f_()
                if st == 1:
                    out_ops.append(S.add("sp", lambda e, h=h, Sh=Sh: e.dma_start(out=sp_d[h], in_=Sh), reads=[r_S], dma_owner=r_S))

                if st == 0:
                    fS, r_fS, ktokS, r_ktokS, qS, r_qS = fSb[p], r_fSb[p], ktokSb[p], r_ktokSb[p], qSb[p], r_qSb[p]

                    def mk_kmask(n):
                        i = n % 2
                        S.add("dve", lambda e, n=n, i=i: e.tensor_scalar(out=kmask[i][0:NS, :], in0=ktokS[0:NS, :], scalar1=identF[0:NS, n:n + 1], scalar2=None, op0=ALU.mult),
                              reads=[r_ktokS, r_cc], writes=[r_kmask[i]])
                    mk_kmask(0)
                    for n in range(NS):
                        i = n % 2
                        if n + 1 < NS:
                            mk_kmask(n + 1)
                        bkv, rbkv = nb()
                        S.add("pe", lambda e, bkv=bkv, i=i: e.matmul(bkv[:, 0:128], lhsT=kmask[i][0:NS, :], rhs=vtokS[0:NS, :], start=True, stop=True), reads=[r_kmask[i], r_vtokS], writes=[rbkv])
                        S.add("dve", lambda e, bkv=bkv, n=n: e.scalar_tensor_tensor(out=Sout[:, n, :], in0=Sin[:, n, :], scalar=fS[:, n:n + 1], in1=bkv[:, 0:128], op0=ALU.mult, op1=ALU.add),
                              reads=[rbkv, r_Sin, r_fS], writes=[r_Sout])
                    bq, rbq = nb()

                    def mmq(e, bq=bq, qS=qS):
                        for n in range(NS):
                            ins = e.matmul(bq[:, n:n + 1], lhsT=Sout[:, n, :], rhs=qS[:, n:n + 1], start=True, stop=True)
                        return ins
                    S.add("pe", mmq, reads=[r_Sout, r_qS], writes=[rbq])
                    S.add("act", lambda e, bq=bq: e.copy(out=oT[:, sc], in_=bq[:, 0:NS]), reads=[rbq], writes=[r_oT])
                    out_ops.append(S.add("sp", lambda e, h=h: e.dma_start(out=ss_d[:, h].rearrange("n d e -> d n e"), in_=Sout[:]), reads=[r_Sout], dma_owner=r_Sout))

                skipB0[0] = (256 if st == 0 else 0)
                S.add("act", lambda e: e.activation(out=sq[:, 0:NT], in_=oT[:, 0:NT], func=AF.Square), reads=[r_oT], writes=[r_sq])
                if h < 7:
                    head_b1(h + 1)
                    skipB0[0] = (256 if st == 0 else 0)
                tl = tiles_of(NT)
                for (c0, n) in tl:
                    bk, rbk = nb()
                    S.add("pe", lambda e, bk=bk, c0=c0, n=n: e.matmul(bk[:, 0:n], lhsT=onesB[:], rhs=sq[:, c0:c0 + n], start=True, stop=True), reads=[r_sq, r_cc], writes=[rbk])
                    S.add("act", lambda e, bk=bk, c0=c0, n=n: e.activation(out=oscr[:, c0:c0 + n], in_=bk[:, 0:n], func=AF.Ln, bias=epsc[:, 3:4]), reads=[rbk, r_cc], writes=[r_oscr])
                lo, hi = tl[0][0], NT
                S.add("act", lambda e, lo=lo, hi=hi: e.activation(out=oscr[:, lo:hi], in_=oscr[:, lo:hi], func=AF.Exp, scale=-0.5), writes=[r_oscr])
                bl_all = sorted(set(b for (c0, n) in tl for b in blocks_of(c0, n)))

                def d1(lo=lo, hi=hi):
                    S.add("dve", lambda e: e.scalar_tensor_tensor(out=oscr[:, lo:hi], in0=oT[:, lo:hi], scalar=pp[:, 58:59], in1=oscr[:, lo:hi], op0=ALU.mult, op1=ALU.mult),
                          reads=[r_oT, r_pp], writes=[r_oscr])

                def d2(lo=lo, hi=hi, h=h, gat=gat, r_gat=r_gat, bl_all=bl_all):
                    S.add("dve", lambda e: e.scalar_tensor_tensor(out=aT[:, h, lo:hi], in0=gat[:, lo:hi], scalar=1.0, in1=oscr[:, lo:hi], op0=ALU.add, op1=ALU.mult),
                          reads=[r_oscr, r_gat], writes=[(r_a if h < 4 else r_u)[b] for b in bl_all])
                pending_on.extend([d1, d2])
            head_b1(0)
            for h in range(8):
                run_head(h)
            while pending_on:
                pending_on.pop(0)()
            wout(NT)

        for st in range(2):
            NT = NTA if st == 0 else NTP
            load_x(st, NT)
            l0_mixer(st, NT)
            ffn(NT, PP["g32f_0"])
            l1_mixer(st, NT)
            ffn(NT, PP["g32f_1"])
            store_y(st, NT)
        assert wstate["taken"] == 2 * NG, (wstate, NG)
        S.add("sp", lambda e: e.nop(), extra_deps=out_ops)

        S.finalize(sems)
        with nc.Block() as block:
            @block.tensor
            def _(e):
                S.run_engine("pe", e)

            @block.scalar
            def _(e):
                S.run_engine("act", e)

            @block.vector
            def _(e):
                S.run_engine("dve", e)

            @block.gpsimd
            def _(e):
                S.run_engine("pool", e)

            @block.sync
            def _(e):
                S.run_engine("sp", e)
    return nc


def kernel(x_prompt, x_sample, cache_k, cache_v, state_hgrn, norm_mix, norm_ffn, w_in_ab, w_out_ab, q_norm, k_norm,
           attn_sink, rel_bias, gmlp_ln_g, gmlp_ln_b, gmlp_w_s, gmlp_b_s, w_in_c, c_lower_bounds, c_out_norm, w_out_c,
           w_gate, w_up, w_down):
    f = lambda a: np.asarray(a, dtype=np.float32)
    x_prompt, x_sample, cache_k, cache_v, state_hgrn = map(f, (x_prompt, x_sample, cache_k, cache_v, state_hgrn))
    wall = build_weight_stream(f(w_in_ab), f(w_out_ab), f(w_in_c), f(w_out_c), f(w_gate), f(w_up), f(w_down))
    NG = wall.shape[0]
    par = build_params(f(norm_mix), f(norm_ffn), f(q_norm), f(k_norm), f(c_lower_bounds), f(c_out_norm), f(gmlp_w_s), f(gmlp_b_s))
    bcast = np.ascontiguousarray(np.broadcast_to(np.concatenate([f(gmlp_ln_g)[0], f(gmlp_ln_b)[0]])[None, :], (128, 1024)))
    ohA, ohS = build_consts()
    wsT = np.ascontiguousarray(f(gmlp_w_s)[0].transpose(2, 0, 1))
    bsr = np.ascontiguousarray(f(gmlp_b_s)[0][None])
    nc = build_program(NG)
    in_maps = []
    for c in range(NCORES):
        b, half = c // 2, c % 2
        t0 = half * 2048
        xin = np.zeros((2 * NTP + NS, D), np.float32)
        lo = t0 - 256
        src_lo = max(lo, 0)
        xin[src_lo - lo:2 * NTP] = x_prompt[b, src_lo:t0 + 2048]
        xin[2 * NTP:] = x_sample[c * NS:(c + 1) * NS, 0]
        in_maps.append(dict(
            xin=xin,
            hm=np.full((128, 1), NEG if half == 0 else 0.0, np.float32),
            ck=np.ascontiguousarray(cache_k[0, c * NS:(c + 1) * NS, 1:].reshape(NS, 127, 128)),
            cv=np.ascontiguousarray(cache_v[0, c * NS:(c + 1) * NS, 1:].reshape(NS, 127, 128)),
            sin=np.ascontiguousarray(state_hgrn[0, c * NS:(c + 1) * NS]),
            wall=wall, par=par, bcast=bcast, relb=f(rel_bias), sink=f(attn_sink).reshape(1, 8),
            ohA=ohA, ohS=ohS, wsT=wsT, bsr=bsr))
    res = run_bass_kernel_spmd(nc, in_maps, core_ids=list(range(NCORES)))
    r = res.results
    y = np.stack([np.concatenate([r[2 * b]["y"], r[2 * b + 1]["y"]], axis=0) for b in range(4)])
    ys = np.concatenate([r[c]["ys"] for c in range(NCORES)], axis=0)[:, None, :]
    nk = np.stack([r[2 * b + 1]["nk"].reshape(128, 2, 64) for b in range(4)])[None]
    nv = np.stack([r[2 * b + 1]["nv"].reshape(128, 2, 64) for b in range(4)])[None]
    nks = np.concatenate([r[c]["nks"] for c in range(NCORES)], axis=0).reshape(1, 128, 128, 2, 64)
    nvs = np.concatenate([r[c]["nvs"] for c in range(NCORES)], axis=0).reshape(1, 128, 128, 2, 64)
    gv = np.stack([r[2 * b + 1]["gv"] for b in range(4)])[None]
    gvs = np.concatenate([r[c]["gvs"] for c in range(NCORES)], axis=0)[None, :, None, :]
    stp = np.stack([r[2 * b + 1]["stp"] for b in range(4)])[None]
    sts = np.concatenate([r[c]["sts"] for c in range(NCORES)], axis=0)[None]
    return tuple(np.ascontiguousarray(a, dtype=np.float32) for a in (y, ys, nk, nv, nks, nvs, gv, gvs, stp, sts))
```
